# Optimizing a Trainium2 kernel written in Bass

```python
import jax, jax.numpy as jnp
from jax import lax
import numpy as np

D_MODEL = 1024
BATCH = 8
SEQ = 4096
DEPTH = 4

EPS = 1e-6
PLE_DIM = 256
D_FF = 2816
SC_WIDTH = D_MODEL
SC_KERNEL = 3
SSM_INNER = 2 * D_MODEL
SSM_HEADDIM = 64
SSM_HEADS = SSM_INNER // SSM_HEADDIM
SSM_GROUPS = 4
SSM_STATE = 128
SSM_CONV = 4
SSM_CHUNK = 128
SSM_CONV_DIM = SSM_INNER + 2 * SSM_GROUPS * SSM_STATE
PROJ_SIZES = (SC_WIDTH, SC_WIDTH, SC_WIDTH,
              SSM_INNER, SSM_CONV_DIM, SSM_HEADS,
              D_MODEL, D_MODEL)
PROJ_DIM = sum(PROJ_SIZES)
PROJ_SPLITS = tuple(int(v) for v in np.cumsum(PROJ_SIZES)[:-1])

kernel_name = "hybrid_shortconv_ssd_macaron_block"


def rmsnorm(x, g):
    xf = x.astype(jnp.float32)
    y = xf * lax.rsqrt(jnp.mean(xf * xf, axis=-1, keepdims=True) + EPS)
    return y.astype(x.dtype) * g


def grouped_rmsnorm(x, g, groups):
    shp = x.shape
    xf = x.astype(jnp.float32).reshape(shp[:-1] + (groups, shp[-1] // groups))
    y = xf * lax.rsqrt(jnp.mean(xf * xf, axis=-1, keepdims=True) + EPS)
    return y.reshape(shp).astype(x.dtype) * g


def swiglu(x, wg, wu, wd):
    return (jax.nn.silu(x @ wg) * (x @ wu)) @ wd


def causal_depthwise_conv(x, w):
    k, c = w.shape
    return lax.conv_general_dilated(
        x, w.reshape(k, 1, c).astype(x.dtype), window_strides=(1,), padding=[(k - 1, 0)],
        dimension_numbers=("NWC", "WIO", "NWC"), feature_group_count=c)


def ssd_chunked(xdt, a, bm, cm):
    b, s, h, p = xdt.shape
    g, n = bm.shape[2], bm.shape[3]
    r = h // g
    nc, L = s // SSM_CHUNK, SSM_CHUNK
    dt_ = xdt.dtype
    X = xdt.reshape(b, nc, L, g, r, p)
    A = a.reshape(b, nc, L, g, r).astype(jnp.float32)
    Bc = bm.reshape(b, nc, L, g, n)
    Cc = cm.reshape(b, nc, L, g, n)
    a_cum = jnp.cumsum(A, axis=2)
    causal = jnp.tril(jnp.ones((L, L), dtype=bool))[None, None, :, :, None, None]
    diff = a_cum[:, :, :, None] - a_cum[:, :, None, :]
    decay = jnp.exp(jnp.where(causal, diff, -jnp.inf)).astype(dt_)
    cb = jnp.einsum("bclgn,bcsgn->bclsg", Cc, Bc)
    y_diag = jnp.einsum("bclsg,bclsgr,bcsgrp->bclgrp", cb, decay, X)
    decay_to_end = jnp.exp(a_cum[:, :, -1:] - a_cum).astype(dt_)
    states = jnp.einsum("bclgn,bclgr,bclgrp->bcgrpn", Bc, decay_to_end, X)
    chunk_decay = jnp.exp(a_cum[:, :, -1]).astype(dt_)

    def step(carry, inp):
        st, dec = inp
        return carry * dec[..., None, None] + st, carry

    init = jnp.zeros((b, g, r, p, n), dtype=states.dtype)
    _, prev = lax.scan(step, init, (jnp.swapaxes(states, 0, 1), jnp.swapaxes(chunk_decay, 0, 1)))
    prev = jnp.swapaxes(prev, 0, 1)
    y_off = jnp.einsum("bclgn,bcgrpn,bclgr->bclgrp", Cc, prev, jnp.exp(a_cum).astype(dt_))
    return (y_diag + y_off).reshape(b, s, h, p)


def hybrid_mixer(u, w_in, sc_conv_w, sc_w_out, m_conv_w, m_conv_b, m_dt_bias, m_A_log, m_D,
                 m_norm, m_w_out, w_o):
    b, s, _ = u.shape
    proj = u @ w_in
    sc_b, sc_c, sc_x, m_z, m_xbc, m_dt, gate_a, gate_m = jnp.split(proj, PROJ_SPLITS, axis=-1)
    y_a = (sc_b * causal_depthwise_conv(sc_c * sc_x, sc_conv_w)) @ sc_w_out
    xbc = jax.nn.silu(causal_depthwise_conv(m_xbc, m_conv_w) + m_conv_b)
    xs, bm, cm = jnp.split(xbc, (SSM_INNER, SSM_INNER + SSM_GROUPS * SSM_STATE), axis=-1)
    dt = jax.nn.softplus((m_dt + m_dt_bias).astype(jnp.float32))
    A = -jnp.exp(m_A_log.astype(jnp.float32))
    xh = xs.reshape(b, s, SSM_HEADS, SSM_HEADDIM)
    y = ssd_chunked(xh * dt.astype(xh.dtype)[..., None], A * dt,
                    bm.reshape(b, s, SSM_GROUPS, SSM_STATE), cm.reshape(b, s, SSM_GROUPS, SSM_STATE))
    y = (y + m_D[:, None] * xh).reshape(b, s, SSM_INNER)
    y_m = grouped_rmsnorm(y * jax.nn.silu(m_z), m_norm, SSM_GROUPS) @ m_w_out
    merged = jax.nn.sigmoid(gate_a) * y_a + jax.nn.sigmoid(gate_m) * y_m
    return merged @ w_o


def setup_inputs(seed: int = 0) -> dict:
    key = jax.random.key(seed)
    ks = iter(jax.random.split(key, 40))

    def nrm(shape, fan_in):
        return jax.random.normal(next(ks), shape, jnp.float32) * (fan_in ** -0.5)

    def gain(shape):
        return 1.0 + 0.05 * jax.random.normal(next(ks), shape, jnp.float32)

    dt0 = jnp.exp(jax.random.uniform(next(ks), (DEPTH, SSM_HEADS), jnp.float32)
                  * (np.log(0.1) - np.log(0.001)) + np.log(0.001))
    return {
        "x": jax.random.normal(next(ks), (BATCH, SEQ, D_MODEL), jnp.float32),
        "p": jax.random.normal(next(ks), (DEPTH, BATCH, SEQ, PLE_DIM), jnp.float32),
        "ffn1_norm": gain((DEPTH, D_MODEL)),
        "ffn1_wg": nrm((DEPTH, D_MODEL, D_FF), D_MODEL),
        "ffn1_wu": nrm((DEPTH, D_MODEL, D_FF), D_MODEL),
        "ffn1_wd": nrm((DEPTH, D_FF, D_MODEL), D_FF),
        "mix_norm": gain((DEPTH, D_MODEL)),
        "w_in": nrm((DEPTH, D_MODEL, PROJ_DIM), D_MODEL),
        "sc_conv_w": nrm((DEPTH, SC_KERNEL, SC_WIDTH), SC_KERNEL),
        "sc_w_out": nrm((DEPTH, SC_WIDTH, D_MODEL), SC_WIDTH),
        "m_conv_w": nrm((DEPTH, SSM_CONV, SSM_CONV_DIM), SSM_CONV),
        "m_conv_b": 0.02 * jax.random.normal(next(ks), (DEPTH, SSM_CONV_DIM), jnp.float32),
        "m_dt_bias": dt0 + jnp.log(-jnp.expm1(-dt0)),
        "m_A_log": jnp.log(jax.random.uniform(next(ks), (DEPTH, SSM_HEADS), jnp.float32, 1.0, 16.0)),
        "m_D": gain((DEPTH, SSM_HEADS)),
        "m_norm": gain((DEPTH, SSM_INNER)),
        "m_w_out": nrm((DEPTH, SSM_INNER, D_MODEL), SSM_INNER),
        "w_o": nrm((DEPTH, D_MODEL, D_MODEL), D_MODEL),
        "ffn2_norm": gain((DEPTH, D_MODEL)),
        "ffn2_wg": nrm((DEPTH, D_MODEL, D_FF), D_MODEL),
        "ffn2_wu": nrm((DEPTH, D_MODEL, D_FF), D_MODEL),
        "ffn2_wd": nrm((DEPTH, D_FF, D_MODEL), D_FF),
        "ple_norm": gain((DEPTH, D_MODEL)),
        "ple_w_gate": nrm((DEPTH, D_MODEL, D_MODEL), D_MODEL),
        "ple_w_proj": nrm((DEPTH, PLE_DIM, D_MODEL), PLE_DIM),
        "final_norm": gain((D_MODEL,)),
    }


def reference(x, p, ffn1_norm, ffn1_wg, ffn1_wu, ffn1_wd, mix_norm, w_in, sc_conv_w, sc_w_out,
              m_conv_w, m_conv_b, m_dt_bias, m_A_log, m_D, m_norm, m_w_out, w_o,
              ffn2_norm, ffn2_wg, ffn2_wu, ffn2_wd, ple_norm, ple_w_gate, ple_w_proj, final_norm):
    h = x
    for i in range(DEPTH):
        h = h + 0.5 * swiglu(rmsnorm(h, ffn1_norm[i]), ffn1_wg[i], ffn1_wu[i], ffn1_wd[i])
        h = h + hybrid_mixer(rmsnorm(h, mix_norm[i]), w_in[i], sc_conv_w[i], sc_w_out[i],
                             m_conv_w[i], m_conv_b[i], m_dt_bias[i], m_A_log[i], m_D[i],
                             m_norm[i], m_w_out[i], w_o[i])
        h = h + 0.5 * swiglu(rmsnorm(h, ffn2_norm[i]), ffn2_wg[i], ffn2_wu[i], ffn2_wd[i])
        gate = jax.nn.sigmoid(rmsnorm(h, ple_norm[i]) @ ple_w_gate[i])
        h = h + gate * (p[i] @ ple_w_proj[i])
    return rmsnorm(h, final_norm)
```

```python
import numpy as np
import os
MK_STOP = float(os.environ.get('MK_STOP', '99'))
from contextlib import ExitStack
import concourse.bass as bass
import concourse.mybir as mybir
from concourse.bass_utils import run_bass_kernel_spmd

F32 = mybir.dt.float32
BF16 = mybir.dt.bfloat16
AF = mybir.ActivationFunctionType
ALU = mybir.AluOpType

D = 1024
DFF = 2816
NFC = DFF // 128
PLE = 256
NH = 32
T = 512
EPS = 1e-6
DEPTH = 4
PROJ = 10272
O_SCB, O_SCC, O_SCX, O_Z, O_XS, O_B, O_C, O_DT, O_GA, O_GM = 0, 1024, 2048, 3072, 5120, 7168, 7680, 8192, 8224, 9248
PL = 288
P_N1, P_NM, P_N2, P_NP, P_SCW, P_MCW, P_MCB, P_MN, P_DTB, P_AL, P_DD = 0, 8, 16, 24, 32, 56, 152, 176, 192, 224, 256
NPAR = DEPTH * PL + 8

SAME_ENGINE_SYNC = True
CONV_INFLIGHT = 2


class Buf:
    __slots__ = ("w", "r", "sem", "cnt", "name", "psum")

    def __init__(self, name=""):
        self.psum = False
        self.w = None
        self.r = []
        self.sem = None
        self.cnt = 0
        self.name = name


class V:
    __slots__ = ("ap", "bufs")

    def __init__(self, ap, bufs):
        self.ap = ap
        self.bufs = bufs if isinstance(bufs, (list, tuple)) else [bufs]


class Tn:
    def __init__(self, t, nb=1, name=""):
        self.t = t
        self.bufs = [Buf(f"{name}{i}") for i in range(nb)]

    def c(self, i, sl=None):
        ap = self.t[:, i] if sl is None else self.t[:, i, sl]
        return V(ap, [self.bufs[i]])

    def all(self):
        return V(self.t[:], self.bufs)

    def v(self, ap, idx=None):
        return V(ap, self.bufs if idx is None else [self.bufs[i] for i in idx])


class GSet:
    def __init__(self, tn_main, off_xs, off_z, tn_bc, off_bc):
        self.m, self.ox, self.oz, self.bc, self.ob = tn_main, off_xs, off_z, tn_bc, off_bc

    def c(self, i, sl=None):
        if i < 4:
            return self.m.c(self.ox + i, sl)
        if i < 6:
            return self.bc.c(self.ob + i - 4, sl)
        return self.m.c(self.oz + i - 6, sl)

    def zview(self, tsl):
        return V(self.m.t[:, self.oz:self.oz + 4, tsl], self.m.bufs[self.oz:self.oz + 4])


class Eng:
    def __init__(self, name, h, sem):
        self.name = name
        self.h = h
        self.sem = sem
        self.cnt = 0
        self.seen = {}


class Prog:
    def __init__(self, nc, es):
        self.nc = nc
        self.es = es
        self.sems = {}
        self.emitted = {}
        self.eng = {}
        for name, h in (("pe", nc.tensor), ("act", nc.scalar), ("dve", nc.vector), ("pool", nc.gpsimd), ("sp", nc.sync)):
            sem = self.newsem("e_" + name)
            self.eng[name] = Eng(name, h, sem)
        self.nwaits = 0
        self.ninst = 0

    def newsem(self, name):
        s = self.es.enter_context(self.nc.semaphore(name))
        self.emitted[id(s)] = 0
        self.sems[id(s)] = s
        return s

    def _wait(self, E, deps):
        best = {}
        for tok in deps:
            if tok is None:
                continue
            sem, val = tok
            k = id(sem)
            if best.get(k, (None, 0))[1] < val:
                best[k] = (sem, val)
        for k, (sem, val) in best.items():
            if sem is E.sem and (E.name in ("pe", "sp") or not SAME_ENGINE_SYNC):
                continue
            if E.seen.get(k, 0) >= val:
                continue
            assert self.emitted[k] >= val, f"wait on unemitted token {E.name} {val} > {self.emitted[k]}"
            E.h.wait_ge(sem, val)
            E.seen[k] = val
            self.nwaits += 1

    def emit(self, eng, fn, reads, writes, signal=True):
        E = self.eng[eng]
        deps = []
        for b in reads:
            deps.append(b.w)
            if b.psum:
                deps.extend(t for t in b.r if t[0] is not E.sem)
        for b in writes:
            deps.append(b.w)
            deps.extend(b.r)
        self._wait(E, deps)
        ins = fn(E.h)
        self.ninst += 1
        tok = (E.sem, E.cnt + 1)
        if signal:
            ins.then_inc(E.sem, 1)
            E.cnt += 1
            self.emitted[id(E.sem)] = E.cnt
        for b in reads:
            b.r.append(tok)
        for b in writes:
            b.w = tok
            b.r = []
        return ins

    def dma(self, q, out, in_, sem_buf=None, nodep=False):
        E = self.eng[q]
        deps = []
        for b in in_.bufs:
            deps.append(b.w)
        for b in out.bufs:
            deps.append(b.w)
            deps.extend(b.r)
        if not nodep:
            self._wait(E, deps)
        sb = sem_buf if sem_buf is not None else out.bufs[0]
        if sb.sem is None:
            sb.sem = self.newsem("d_" + sb.name)
        ins = E.h.dma_start(out=out.ap, in_=in_.ap)
        ins.then_inc(sb.sem, 16)
        sb.cnt += 16
        self.emitted[id(sb.sem)] = sb.cnt
        tok = (sb.sem, sb.cnt)
        for b in in_.bufs:
            b.r.append(tok)
        for b in out.bufs:
            b.w = tok
            b.r = []
        self.ninst += 1
        return tok

    def mm(self, out, lhsT, rhs, start=True, stop=True, signal=True):
        return self.emit("pe", lambda e: e.matmul(out.ap, lhsT=lhsT.ap, rhs=rhs.ap, start=start, stop=stop),
                         lhsT.bufs + rhs.bufs, out.bufs, signal)

    def tr(self, out, in_, ident, signal=True):
        return self.emit("pe", lambda e: e.transpose(out.ap, in_.ap, ident.ap), in_.bufs + ident.bufs, out.bufs, signal)

    def act(self, out, in_, func, bias=None, scale=1.0):
        rd = list(in_.bufs)
        kw = {}
        if bias is not None:
            if isinstance(bias, V):
                rd += bias.bufs
                kw["bias"] = bias.ap
            else:
                kw["bias"] = float(bias)
        if isinstance(scale, V):
            rd += scale.bufs
            kw["scale"] = scale.ap
        else:
            kw["scale"] = float(scale)
        return self.emit("act", lambda e: e.activation(out=out.ap, in_=in_.ap, func=func, **kw), rd, out.bufs)

    def tt(self, eng, out, in0, in1, op):
        return self.emit(eng, lambda e: e.tensor_tensor(out=out.ap, in0=in0.ap, in1=in1.ap, op=op),
                         in0.bufs + in1.bufs, out.bufs)

    def ts(self, eng, out, in0, s1, s2, op0, op1=None):
        rd = list(in0.bufs)
        a1 = s1
        a2 = s2
        if isinstance(s1, V):
            rd += s1.bufs
            a1 = s1.ap
        if isinstance(s2, V):
            rd += s2.bufs
            a2 = s2.ap
        if op1 is None:
            return self.emit(eng, lambda e: e.tensor_scalar(out=out.ap, in0=in0.ap, scalar1=a1, scalar2=None, op0=op0), rd, out.bufs)
        return self.emit(eng, lambda e: e.tensor_scalar(out=out.ap, in0=in0.ap, scalar1=a1, scalar2=a2, op0=op0, op1=op1), rd, out.bufs)

    def stt(self, out, in0, scalar, in1, op0, op1):
        rd = in0.bufs + in1.bufs
        a = scalar
        if isinstance(scalar, V):
            rd = rd + scalar.bufs
            a = scalar.ap
        return self.emit("dve", lambda e: e.scalar_tensor_tensor(out=out.ap, in0=in0.ap, scalar=a, in1=in1.ap, op0=op0, op1=op1),
                         rd, out.bufs)

    def copy(self, eng, out, in_):
        if eng == "act":
            return self.emit("act", lambda e: e.activation(out=out.ap, in_=in_.ap, func=AF.Copy), in_.bufs, out.bufs)
        return self.emit(eng, lambda e: e.tensor_copy(out=out.ap, in_=in_.ap), in_.bufs, out.bufs)

    def recip(self, out, in_):
        return self.emit("dve", lambda e: e.reciprocal(out=out.ap, in_=in_.ap), in_.bufs, out.bufs)

    def memset(self, eng, out, val):
        return self.emit(eng, lambda e: e.memset(out.ap, val), [], out.bufs)


class Ring:
    def __init__(self, items):
        self.items = items
        self.i = 0

    def next(self):
        x = self.items[self.i % len(self.items)]
        self.i += 1
        return x


def _blocks(c0, n, step=512):
    out = []
    c = 0
    while c < n:
        out.append((c0 + c, min(step, n - c)))
        c += step
    return out


WSPEC = {}
for _f in ("ffn1", "ffn2"):
    WSPEC[_f + "_wg"] = (_f + "_wg", D, _blocks(0, DFF))
    WSPEC[_f + "_wu"] = (_f + "_wu", D, _blocks(0, DFF))
    WSPEC[_f + "_wd"] = (_f + "_wd", DFF, _blocks(0, D))
WSPEC["dt"] = ("w_in", D, [(O_DT, 32)])
WSPEC["xs"] = ("w_in", D, _blocks(O_XS, 2048))
WSPEC["B"] = ("w_in", D, _blocks(O_B, 512, 128))
WSPEC["C"] = ("w_in", D, _blocks(O_C, 512, 128))
WSPEC["z"] = ("w_in", D, _blocks(O_Z, 2048))
WSPEC["scb"] = ("w_in", D, _blocks(O_SCB, 1024))
WSPEC["scc"] = ("w_in", D, _blocks(O_SCC, 1024))
WSPEC["scx"] = ("w_in", D, _blocks(O_SCX, 1024))
WSPEC["ga"] = ("w_in", D, _blocks(O_GA, 1024))
WSPEC["gm"] = ("w_in", D, _blocks(O_GM, 1024))
WSPEC["scwo"] = ("sc_w_out", D, _blocks(0, D))
WSPEC["mwo"] = ("m_w_out", 2048, _blocks(0, D))
WSPEC["wo"] = ("w_o", D, _blocks(0, D))
WSPEC["pleg"] = ("ple_w_gate", D, _blocks(0, D))
WSPEC["plep"] = ("ple_w_proj", PLE, _blocks(0, D))

WSHAPES = {"ffn1_wg": (D, DFF), "ffn1_wu": (D, DFF), "ffn1_wd": (DFF, D), "ffn2_wg": (D, DFF), "ffn2_wu": (D, DFF),
           "ffn2_wd": (DFF, D), "w_in": (D, PROJ), "sc_w_out": (D, D), "m_w_out": (2048, D), "w_o": (D, D),
           "ple_w_gate": (D, D), "ple_w_proj": (PLE, D)}
WORDER = ["ffn1_wg", "ffn1_wu", "ffn1_wd", "dt", "xs", "B", "C", "z", "scc", "scx", "scb", "scwo", "mwo", "ga", "gm", "wo",
          "ffn2_wg", "ffn2_wu", "ffn2_wd", "pleg", "plep"]


def build_program(S, layers, do_final, nslab=5, phases=("ffn1", "mixer", "ffn2", "ple")):
    NT = S // T
    nc = bass.Bass("TRN2", target_bir_lowering=False)
    es = ExitStack()
    with es:
        P = Prog(nc, es)

        def dram(name, shape, dt, kind):
            return nc.dram_tensor(name, list(shape), dt, kind=kind).ap()

        def sb(name, shape, dt):
            return es.enter_context(nc.sbuf_tensor(name, list(shape), dt))

        def pp(name, shape, dt):
            return es.enter_context(nc.psum_tensor(name, list(shape), dt))

        x_d = dram("x", (S, D), F32, "ExternalInput")
        p_d = dram("p", (DEPTH, S, PLE), F32, "ExternalInput")
        par_d = dram("params", (128, NPAR), F32, "ExternalInput")
        cst_d = dram("consts", (128, 5, 128), F32, "ExternalInput")
        out_d = dram("out", (S, D), F32, "ExternalOutput")
        w_d = {n: dram(n, (len(layers),) + shp, F32, "ExternalInput") for n, shp in WSHAPES.items()}
        xbuf_d = Buf("x_d")
        outbuf_d = Buf("out_d")

        scr = {}
        for l in layers:
            for name in WORDER:
                src, K, blks = WSPEC[name]
                kc = K // 128
                t = dram(f"scr_{l}_{name}", (len(blks), 128, kc, 512), BF16, "Internal")
                scr[(l, name)] = (t, Buf(f"scr{l}{name}"), kc, blks)

        resid = Tn(sb("resid", (128, 8, T), F32), 8, "res")
        xn = Tn(sb("xn", (128, 8, T), BF16), 8, "xn")
        actp = Tn(sb("actp", (128, 24, T), BF16), 24, "act")
        grpA = Tn(sb("grpA", (128, 10, T), BF16), 10, "grpA")
        bc2 = Tn(sb("bc2", (128, 2, T), BF16), 2, "bc2")
        pT = bc2
        gy = Tn(sb("gy", (128, 4, T), F32), 4, "gy")
        states = {l: Tn(sb(f"st{l}", (128, 4, T), F32), 4, f"st{l}") for l in layers}
        sbf = Tn(sb("sbf", (128, 4, T), BF16), 4, "sbf")
        tailm = {l: Tn(sb(f"tm{l}", (128, 24, 3), F32), 24, f"tm{l}") for l in layers}
        tails = {l: Tn(sb(f"ts{l}", (128, 8, 2), F32), 8, f"ts{l}") for l in layers}
        cvr = Ring([Tn(sb(f"cv{i}", (128, 1, T + 3), F32), 1, f"cv{i}") for i in range(2)])
        f32r = Ring([Tn(sb(f"f32r{i}", (128, 1, T), F32), 1, f"f32r{i}") for i in range(4)])
        rstd = Tn(sb("rstd", (128, 1, T), F32), 1, "rstd")
        slabs = [Tn(sb(f"slab{i}", (128, 8, 512), BF16), 1, f"slab{i}") for i in range(nslab)]
        params = Tn(sb("params_sb", (128, 1, NPAR), F32), 1, "params")
        consts = Tn(sb("consts_sb", (128, 5, 128), F32), 1, "consts")
        identb = Tn(sb("identb", (128, 1, 128), BF16), 1, "identb")
        onesb = Tn(sb("onesb", (128, 2, 128), BF16), 1, "onesb")
        cmb = Tn(sb("cmb", (128, 1, 128), F32), 1, "cmb")
        aneg = Tn(sb("aneg", (128, 1, DEPTH * NH), F32), 1, "aneg")
        xstg = Tn(sb("xstg", (128, 1, D), F32), 1, "xstg")
        dtt = {n: Tn(sb("dt_" + n, (128, 4, NH), F32), 1, "dt_" + n) for n in
               ("t", "ab", "ex", "dt", "a", "acum", "dif", "dte", "cd")}
        dtt["dtdte"] = dtt["t"]
        dtt["E"] = dtt["ab"]
        Xb = Tn(sb("Xb", (128, 4, T), BF16), 4, "Xb")
        Xdb = Tn(sb("Xdb", (128, 3, T), BF16), 3, "Xdb")
        xsD = Tn(sb("xsD", (128, 4, T), BF16), 4, "xsD")
        BTb = Tn(sb("BTb", (128, 3, 128), BF16), 3, "BTb")
        CBm = Tn(sb("CBm", (128, 3, 128), BF16), 3, "CBm")
        Lf0 = Tn(sb("Lf0", (128, 8, 128), F32), 1, "Lf0")
        Lf1 = Tn(xstg.t[:, 0, :].rearrange("p (h s) -> p h s", h=8), 1, "Lf1")
        Lf1.bufs = xstg.bufs
        Lfr = Ring([Lf0, Lf1])
        dec = Tn(sb("dec", (128, 4, 1024), BF16), 4, "dec")
        Mh = dec
        ybf = Tn(sb("ybf", (128, 2, T), BF16), 2, "ybf")
        accr = Ring([Tn(sb(f"accr{i}", (128, 1, T), F32), 1, f"accr{i}") for i in range(2)])
        print("[build] sbuf bytes remaining", nc.sbuf_bytes_remaining)

        ps6 = [Tn(pp(f"psf{i}", (128, 1, 512), F32), 1, f"psf{i}") for i in range(6)]
        psf = Ring(ps6)
        pss = Ring(ps6[:4])
        psp = Ring(ps6[4:])
        psb = Ring([Tn(pp(f"psb{i}", (128, 1, 1024), BF16), 1, f"psb{i}") for i in range(2)])
        for tn_ in psf.items + psb.items:
            tn_.bufs[0].psum = True

        conv_hist = []
        PH_NAMES = {"ffn1": ["ffn1_wg", "ffn1_wu", "ffn1_wd"],
                    "mixer": ["dt", "xs", "B", "C", "z", "scc", "scx", "scb", "scwo", "mwo", "ga", "gm", "wo"],
                    "ffn2": ["ffn2_wg", "ffn2_wu", "ffn2_wd"], "ple": ["pleg", "plep"]}
        conv_seq = [(l, ph) for l in layers for ph in ("ffn1", "mixer", "ffn2", "ple")]
        conv_state = {"pos": 0}

        def conv_ahead(n):
            while conv_state["pos"] < min(len(conv_seq), n):
                l, ph = conv_seq[conv_state["pos"]]
                conv_state["pos"] += 1
                for name in PH_NAMES[ph]:
                    if name not in WORDER:
                        continue
                    src, K, blks = WSPEC[name]
                    t, buf, kc, _ = scr[(l, name)]
                    if len(conv_hist) >= CONV_INFLIGHT:
                        pb_ = conv_hist[-CONV_INFLIGHT]
                        P.eng["pool"].h.wait_ge(pb_.sem, pb_.cnt)
                    conv_hist.append(buf)
                    for bi, (c0, ncol) in enumerate(blks):
                        for k0 in range(0, kc, 8):
                            k1 = min(kc, k0 + 8)
                            src_ap = w_d[src][layers.index(l), k0 * 128:k1 * 128, c0:c0 + ncol].rearrange("(k p) c -> p k c", p=128)
                            P.dma("pool", V(t[bi, :, k0:k1, 0:ncol], [buf]), V(src_ap, []), nodep=True)

        class WS:
            def __init__(self):
                self.reqs = []
                self.dry = True
                self.i = 0
                self.loaded = 0
                self.released = 0

            def get(self, l, name, bi, k0, k1):
                if self.dry:
                    self.reqs.append((l, name, bi, k0, k1))
                    return None
                i = self.i
                assert self.reqs[i] == (l, name, bi, k0, k1)
                assert i - nslab < self.released, "too many slabs held"
                self._pump()
                assert self.loaded > i
                self.i += 1
                return slabs[i % nslab]

            def _pump(self):
                while self.loaded < min(len(self.reqs), self.released + nslab):
                    self._load(self.loaded)
                    self.loaded += 1

            def release(self):
                if self.dry:
                    return
                self.released = self.i
                self._pump()

            def _load(self, j):
                l, name, bi, k0, k1 = self.reqs[j]
                t, buf, kc, blks = scr[(l, name)]
                ncol = blks[bi][1]
                sl = slabs[j % nslab]
                P.dma("sp", V(sl.t[:, 0:k1 - k0, 0:ncol], sl.bufs), V(t[bi, :, k0:k1, 0:ncol], [buf]))

        ws = WS()

        def par(col, n=1):
            return V(params.t[:, 0, col:col + n], params.bufs)

        stats = {"ps": None, "n": 0}

        def stats_sq(chunks):
            for c in chunks:
                P.act(xn.c(c), resid.c(c), AF.Square)

        def stats_mm(chunks):
            for c in chunks:
                if stats["n"] == 0:
                    stats["ps"] = psf.next()
                P.mm(stats["ps"].c(0), V(onesb.t[:, 0, :], onesb.bufs), xn.c(c), start=(stats["n"] == 0), stop=(stats["n"] == 7))
                stats["n"] += 1

        def rmsnorm(l_col, out_f32=None):
            if ws.dry:
                return
            if stats["n"] == 0:
                stats_sq(range(8))
                stats_mm(range(8))
            assert stats["n"] == 8
            ps = stats["ps"]
            stats["n"] = 0
            stats["ps"] = None
            rstd_from(ps)
            for c in range(8):
                o = xn.c(c) if out_f32 is None else out_f32(c)
                P.stt(o, resid.c(c), par(l_col + c), rstd.c(0), ALU.mult, ALU.mult)

        def ffn(l, pre, ncol_norm):
            rmsnorm(l * PL + ncol_norm)
            blks = WSPEC[pre + "_wg"][2]
            for bi, (c0, ncol) in enumerate(blks):
                wg = ws.get(l, pre + "_wg", bi, 0, 8)
                wu = ws.get(l, pre + "_wu", bi, 0, 8)
                if ws.dry:
                    continue
                for fc in range(ncol // 128):
                    j = bi * 4 + fc
                    pg = psf.next()
                    for k in range(8):
                        P.mm(pg.c(0), V(wg.t[:, k, fc * 128:(fc + 1) * 128], wg.bufs), xn.c(k), start=(k == 0), stop=(k == 7))
                    pu = psf.next()
                    for k in range(8):
                        P.mm(pu.c(0), V(wu.t[:, k, fc * 128:(fc + 1) * 128], wu.bufs), xn.c(k), start=(k == 0), stop=(k == 7))
                    sg = f32r.next()
                    P.act(sg.c(0), pg.c(0), AF.Silu)
                    P.tt("dve", actp.c(j), sg.c(0), pu.c(0), ALU.mult)
                ws.release()
            for cb in range(2):
                pss = [psf.next() for _ in range(4)] if not ws.dry else None
                for k0 in range(0, NFC, 8):
                    k1 = min(NFC, k0 + 8)
                    wd = ws.get(l, pre + "_wd", cb, k0, k1)
                    if ws.dry:
                        continue
                    for kk in range(k1 - k0):
                        j = k0 + kk
                        for m4 in range(4):
                            P.mm(pss[m4].c(0), V(wd.t[:, kk, m4 * 128:(m4 + 1) * 128], wd.bufs), actp.c(j),
                                 start=(j == 0), stop=(j == NFC - 1))
                    ws.release()
                if ws.dry:
                    continue
                if cb == 1:
                    stats_mm(range(0, 4))
                for m4 in range(4):
                    m = cb * 4 + m4
                    P.stt(resid.c(m), pss[m4].c(0), 0.5, resid.c(m), ALU.mult, ALU.add)
                stats_sq(range(cb * 4, cb * 4 + 4))
                if cb == 1:
                    stats_mm(range(4, 8))

        def proj(slab, fc, out_ps):
            for k in range(8):
                P.mm(out_ps, V(slab.t[:, k, fc * 128:(fc + 1) * 128], slab.bufs), xn.c(k), start=(k == 0), stop=(k == 7))

        def bc_h(tn, tc, g):
            return V(tn.t[:, tc, g * 8:(g + 1) * 8].unsqueeze(2).to_broadcast([128, 8, 64]), tn.bufs)

        def rstd_from(ps):
            s_ = f32r.next()
            P.act(s_.c(0), ps.c(0), AF.Ln, bias=EPS)
            P.act(rstd.c(0), s_.c(0), AF.Exp, scale=-0.5)

        def proj_get(l, g):
            return (ws.get(l, "xs", g, 0, 8), ws.get(l, "B", g, 0, 8), ws.get(l, "C", g, 0, 8), ws.get(l, "z", g, 0, 8))

        def proj_units(l, g, G, sl4):
            pb = l * PL
            wxs, wB, wC, wz = sl4
            pend = []
            for i in range(6):
                if i < 4:
                    slab, fc, ch = wxs, i, g * 4 + i
                elif i == 4:
                    slab, fc, ch = wB, 0, 16 + g
                else:
                    slab, fc, ch = wC, 0, 20 + g
                ps = psp.next()
                proj(slab, fc, ps.c(0))
                cv = cvr.next()
                tl = tailm[l]
                wc = pb + P_MCW + ch * 4
                acc = accr.next()
                P.act(acc.c(0), ps.c(0), AF.Identity, bias=par(pb + P_MCB + ch), scale=par(wc + 3))
                P.copy("pool", V(cv.t[:, 0, 0:3], cv.bufs), V(tl.t[:, ch, :], [tl.bufs[ch]]))
                P.copy("act", V(cv.t[:, 0, 3:3 + T], cv.bufs), ps.c(0))
                P.copy("pool", V(tl.t[:, ch, :], [tl.bufs[ch]]), V(cv.t[:, 0, T:T + 3], cv.bufs))
                p2 = f32r.next()
                P.act(p2.c(0), V(cv.t[:, 0, 2:2 + T], cv.bufs), AF.Identity, scale=par(wc + 2))
                for (o_, a_) in pend:
                    P.act(o_, a_, AF.Silu)
                pend = [(G.c(i), acc.c(0))]
                P.tt("pool", acc.c(0), acc.c(0), p2.c(0), ALU.add)
                for k in range(0, 2):
                    P.stt(acc.c(0), V(cv.t[:, 0, k:k + T], cv.bufs), par(wc + k), acc.c(0), ALU.mult, ALU.add)
                yield
            for fc in range(4):
                ps = psp.next()
                proj(wz, fc, ps.c(0))
                for (o_, a_) in pend:
                    P.act(o_, a_, AF.Silu)
                pend = []
                P.act(G.c(6 + fc), ps.c(0), AF.Silu)
                if fc == 3:
                    ws.release()
                yield

        def silu_batch(G):
            for i in range(10):
                P.act(G.c(i), G.c(i), AF.Silu)

        def mixer(l, st):
            pb = l * PL
            rmsnorm(pb + P_NM)
            d = dtt
            v4 = lambda n: V(d[n].t[:], d[n].bufs)
            gsets = [GSet(grpA, 0, 6, grpA, 4), GSet(actp, 16, 20, bc2, 0)]
            wdt = ws.get(l, "dt", 0, 0, 8)
            if not ws.dry:
                ps = psf.next()
                for tc in range(4):
                    for k in range(8):
                        P.mm(V(ps.t[:, 0, tc * 32:(tc + 1) * 32], ps.bufs), V(xn.t[:, k, tc * 128:(tc + 1) * 128], [xn.bufs[k]]),
                             V(wdt.t[:, k, 0:32], wdt.bufs), start=(k == 0), stop=(k == 7))
                ps4 = V(ps.t[:, 0, 0:128].rearrange("p (c h) -> p c h", c=4), ps.bufs)
                bias_b = V(params.t[:, 0, pb + P_DTB:pb + P_DTB + NH].unsqueeze(1).to_broadcast([128, 4, NH]), params.bufs)
                P.tt("dve", v4("t"), ps4, bias_b, ALU.add)
                ws.release()
                P.act(v4("ab"), v4("t"), AF.Abs)
                P.act(v4("ex"), v4("ab"), AF.Exp, scale=-1.0)
                P.act(v4("ab"), v4("ex"), AF.Ln, bias=1.0)
                P.stt(v4("dt"), v4("t"), 0.0, v4("ab"), ALU.max, ALU.add)
                an_b = V(aneg.t[:, 0, l * NH:(l + 1) * NH].unsqueeze(1).to_broadcast([128, 4, NH]), aneg.bufs)
                P.tt("dve", v4("a"), v4("dt"), an_b, ALU.mult)
            sl4 = proj_get(l, 0)
            if not ws.dry:
                for _ in proj_units(l, 0, gsets[0], sl4):
                    pass
                pa = psf.next()
                pt = psf.next()
                for tc in range(4):
                    P.mm(V(pa.t[:, 0, tc * 32:(tc + 1) * 32], pa.bufs), V(consts.t[:, 1, :], consts.bufs),
                         V(d["a"].t[:, tc, :], d["a"].bufs))
                    P.mm(V(pt.t[:, 0, tc * 32:(tc + 1) * 32], pt.bufs), V(consts.t[:, 4, :], consts.bufs),
                         V(d["a"].t[:, tc, :], d["a"].bufs))
                pa4 = V(pa.t[:, 0, 0:128].rearrange("p (c h) -> p c h", c=4), pa.bufs)
                pt4 = V(pt.t[:, 0, 0:128].rearrange("p (c h) -> p c h", c=4), pt.bufs)
                P.copy("dve", v4("acum"), pa4)
                P.act(v4("E"), v4("acum"), AF.Exp)
                P.tt("dve", v4("dif"), pt4, v4("acum"), ALU.subtract)
                P.act(v4("dte"), v4("dif"), AF.Exp)
                P.tt("dve", v4("ex"), v4("dif"), v4("acum"), ALU.add)
                P.act(v4("cd"), v4("ex"), AF.Exp)
                P.tt("dve", v4("dtdte"), v4("dt"), v4("dte"), ALU.mult)

            pend_norm = {"g": None}

            def group_norm():
                gg = pend_norm["g"]
                if gg is None or ws.dry:
                    return
                pend_norm["g"] = None
                ps = pss.next()
                for fc in range(4):
                    P.act(actp.c(gg * 4 + fc), gy.c(fc), AF.Square)
                    P.mm(ps.c(0), V(onesb.t[:, 1, :], onesb.bufs), actp.c(gg * 4 + fc), start=(fc == 0), stop=(fc == 3))
                rstd_from(ps)
                for fc in range(4):
                    P.stt(actp.c(gg * 4 + fc), gy.c(fc), par(pb + P_MN + gg * 4 + fc), rstd.c(0), ALU.mult, ALU.mult)

            for g in range(4):
                G = gsets[g % 2]
                gen = None
                if g < 3:
                    sl4 = proj_get(l, g + 1)
                    if not ws.dry:
                        gen = proj_units(l, g + 1, gsets[(g + 1) % 2], sl4)
                if ws.dry:
                    continue

                def pump(n=1):
                    if gen is not None:
                        for _ in range(n):
                            next(gen, None)

                S_ = states[l]
                v3 = lambda tn, i: V(tn.t[:, i, :].rearrange("p (h d) -> p h d", h=8), [tn.bufs[i]])
                P.copy("act", sbf.c(0), S_.c(g))
                Lfs = {}
                pxss = {}

                def t_evac(tc):
                    tsl = slice(tc * 128, (tc + 1) * 128)
                    pxs = psb.next()
                    for fc in range(4):
                        P.tr(V(pxs.t[:, 0, fc * 128:(fc + 1) * 128], pxs.bufs), G.c(fc, tsl), identb.c(0))
                    P.tr(V(pxs.t[:, 0, 512:640], pxs.bufs), G.c(4, tsl), identb.c(0))
                    pcb = pss.next()
                    P.mm(V(pcb.t[:, 0, 0:128], pcb.bufs), G.c(4, tsl), G.c(5, tsl))
                    pxs3 = V(pxs.t[:, 0, 0:T].rearrange("p (h d) -> p h d", h=8), pxs.bufs)
                    P.tt("dve", v3(Xb, tc), pxs3, bc_h(d["dt"], tc, g), ALU.mult)
                    P.tt("dve", v3(Xdb, tc % 3), pxs3, bc_h(d["dtdte"], tc, g), ALU.mult)
                    dd_b = V(params.t[:, 0, pb + P_DD + g * 8:pb + P_DD + (g + 1) * 8].unsqueeze(2).to_broadcast([128, 8, 64]), params.bufs)
                    P.tt("dve", v3(xsD, tc), pxs3, dd_b, ALU.mult)
                    P.copy("dve", BTb.c(tc % 3), V(pxs.t[:, 0, 512:640], pxs.bufs))
                    P.tt("dve", CBm.c(tc % 3), V(pcb.t[:, 0, 0:128], pcb.bufs), cmb.c(0), ALU.mult)

                def mk_Lf(tc):
                    Lf = Lfr.next()
                    Lfs[tc] = Lf
                    ms_b = V(consts.t[:, 2, :].unsqueeze(1).to_broadcast([128, 8, 128]), consts.bufs)
                    a_b = V(d["a"].t[:, tc, g * 8:(g + 1) * 8].unsqueeze(2).to_broadcast([128, 8, 128]), d["a"].bufs)
                    P.tt("pool", V(Lf.t[:], Lf.bufs), ms_b, a_b, ALU.mult)

                def decay(tc):
                    Lf = Lfs[tc]
                    for hb in range(2):
                        pd = pss.next()
                        for hh in range(4):
                            h = hb * 4 + hh
                            P.mm(V(pd.t[:, 0, hh * 128:(hh + 1) * 128], pd.bufs), V(Lf.t[:, h, :], Lf.bufs),
                                 V(consts.t[:, 1, :], consts.bufs))
                        P.act(V(dec.t[:, tc, hb * 512:(hb + 1) * 512].rearrange("p (h s) -> p h s", h=4), [dec.bufs[tc]]),
                              V(pd.t[:, 0, :].rearrange("p (h s) -> p h s", h=4), pd.bufs), AF.Exp)
                    cb_b = V(CBm.t[:, tc % 3, :].unsqueeze(1).to_broadcast([128, 8, 128]), [CBm.bufs[tc % 3]])
                    P.tt("dve", V(Mh.t[:, tc, :].rearrange("p (h s) -> p h s", h=8), [Mh.bufs[tc]]),
                         V(dec.t[:, tc, :].rearrange("p (h s) -> p h s", h=8), [dec.bufs[tc]]), cb_b, ALU.mult)

                def state(tc):
                    pst = pss.next()
                    P.mm(pst.c(0), BTb.c(tc % 3), Xdb.c(tc % 3))
                    s3 = V(S_.t[:, g, :].rearrange("p (h d) -> p h d", h=8), [S_.bufs[g]])
                    P.tt("pool", s3, s3, bc_h(d["cd"], tc, g), ALU.mult)
                    P.tt("dve", S_.c(g), S_.c(g), pst.c(0), ALU.add)
                    if tc < 3:
                        P.copy("act", sbf.c(tc + 1), S_.c(g))

                def ycomb(tc):
                    tsl = slice(tc * 128, (tc + 1) * 128)
                    py = pss.next()
                    for h in range(8):
                        P.mm(V(py.t[:, 0, h * 64:(h + 1) * 64], py.bufs), V(Mh.t[:, tc, h * 128:(h + 1) * 128], [Mh.bufs[tc]]),
                             V(Xb.t[:, tc, h * 64:(h + 1) * 64], [Xb.bufs[tc]]))
                    pyo = pss.next()
                    P.mm(pyo.c(0), G.c(5, tsl), sbf.c(tc))
                    t1 = f32r.next()
                    pyo3 = V(pyo.t[:, 0, :].rearrange("p (h d) -> p h d", h=8), pyo.bufs)
                    P.tt("dve", V(t1.t[:, 0, :].rearrange("p (h d) -> p h d", h=8), t1.bufs), pyo3, bc_h(d["E"], tc, g), ALU.mult)
                    P.tt("dve", t1.c(0), t1.c(0), py.c(0), ALU.add)
                    P.tt("pool", ybf.c(tc % 2), t1.c(0), xsD.c(tc), ALU.add)

                def ygate(tc):
                    tsl = slice(tc * 128, (tc + 1) * 128)
                    pyt = psb.next()
                    for fc in range(4):
                        P.tr(V(pyt.t[:, 0, fc * 128:(fc + 1) * 128], pyt.bufs), V(ybf.t[:, tc % 2, fc * 128:(fc + 1) * 128], [ybf.bufs[tc % 2]]),
                             identb.c(0))
                    P.tt("dve", V(gy.t[:, :, tsl], gy.bufs), V(pyt.t[:, 0, 0:T].rearrange("p (f t) -> p f t", f=4), pyt.bufs),
                         G.zview(tsl), ALU.mult)

                t_evac(0)
                t_evac(1)
                mk_Lf(0)
                for tc in range(4):
                    if tc + 2 < 4:
                        t_evac(tc + 2)
                    if tc + 1 < 4:
                        mk_Lf(tc + 1)
                    decay(tc)
                    state(tc)
                    if tc == 1:
                        group_norm()
                for tc in range(4):
                    ycomb(tc)
                    pump(2)
                    if tc >= 1:
                        ygate(tc - 1)
                pump(10)
                ygate(3)
                pend_norm["g"] = g

            for cb in range(2):
                wc_ = ws.get(l, "scc", cb, 0, 8)
                wx_ = ws.get(l, "scx", cb, 0, 8)
                wb_ = ws.get(l, "scb", cb, 0, 8)
                if ws.dry:
                    continue
                for fc in range(4):
                    ch = cb * 4 + fc
                    pc = psf.next()
                    proj(wc_, fc, pc.c(0))
                    px = psf.next()
                    proj(wx_, fc, px.c(0))
                    pbb = psf.next()
                    proj(wb_, fc, pbb.c(0))
                    csb = f32r.next()
                    P.copy("act", csb.c(0), pc.c(0))
                    cv = cvr.next()
                    tl = tails[l]
                    P.copy("pool", V(cv.t[:, 0, 0:2], cv.bufs), V(tl.t[:, ch, :], [tl.bufs[ch]]))
                    P.tt("dve", V(cv.t[:, 0, 2:2 + T], cv.bufs), csb.c(0), px.c(0), ALU.mult)
                    P.copy("pool", V(tl.t[:, ch, :], [tl.bufs[ch]]), V(cv.t[:, 0, T:T + 2], cv.bufs))
                    acc = f32r.next()
                    wc = pb + P_SCW + ch * 3
                    P.act(acc.c(0), V(cv.t[:, 0, 0:T], cv.bufs), AF.Identity, scale=par(wc))
                    for k in range(1, 3):
                        P.stt(acc.c(0), V(cv.t[:, 0, k:k + T], cv.bufs), par(wc + k), acc.c(0), ALU.mult, ALU.add)
                    P.tt("dve", actp.c(16 + ch), acc.c(0), pbb.c(0), ALU.mult)
                ws.release()
                group_norm()

            for cb in range(2):
                wsc = ws.get(l, "scwo", cb, 0, 8)
                wga = ws.get(l, "ga", cb, 0, 8)
                if not ws.dry:
                    for m4 in range(4):
                        m = cb * 4 + m4
                        msl = slice(m4 * 128, (m4 + 1) * 128)
                        pga = psf.next()
                        proj(wga, m4, pga.c(0))
                        pya = psf.next()
                        for k in range(8):
                            P.mm(pya.c(0), V(wsc.t[:, k, msl], wsc.bufs), actp.c(16 + k), start=(k == 0), stop=(k == 7))
                        sa = f32r.next()
                        P.act(sa.c(0), pga.c(0), AF.Sigmoid)
                        P.tt("dve", grpA.c(m), sa.c(0), pya.c(0), ALU.mult)
                    ws.release()
                wm0 = ws.get(l, "mwo", cb, 0, 8)
                wm1 = ws.get(l, "mwo", cb, 8, 16)
                wgm = ws.get(l, "gm", cb, 0, 8)
                if ws.dry:
                    continue
                for m4 in range(4):
                    m = cb * 4 + m4
                    msl = slice(m4 * 128, (m4 + 1) * 128)
                    pgm = psf.next()
                    proj(wgm, m4, pgm.c(0))
                    pym = psf.next()
                    for k in range(16):
                        wm = wm0 if k < 8 else wm1
                        P.mm(pym.c(0), V(wm.t[:, k % 8, msl], wm.bufs), actp.c(k), start=(k == 0), stop=(k == 15))
                    sm = f32r.next()
                    P.act(sm.c(0), pgm.c(0), AF.Sigmoid)
                    P.tt("dve", sm.c(0), sm.c(0), pym.c(0), ALU.mult)
                    P.tt("pool", grpA.c(m), grpA.c(m), sm.c(0), ALU.add)
                ws.release()
            for cb in range(2):
                wo = ws.get(l, "wo", cb, 0, 8)
                if ws.dry:
                    continue
                for m4 in range(4):
                    m = cb * 4 + m4
                    po = psf.next()
                    for k in range(8):
                        P.mm(po.c(0), V(wo.t[:, k, m4 * 128:(m4 + 1) * 128], wo.bufs), grpA.c(k), start=(k == 0), stop=(k == 7))
                    P.tt("dve", resid.c(m), po.c(0), resid.c(m), ALU.add)
                ws.release()
                if cb == 1:
                    stats_mm(range(0, 4))
                stats_sq(range(cb * 4, cb * 4 + 4))
                if cb == 1:
                    stats_mm(range(4, 8))

        def ple(l, st):
            pb = l * PL
            rmsnorm(pb + P_NP)
            if not ws.dry:
                pstg_ap = xstg.t[:, 0, :].rearrange("p (c f) -> p c f", c=4)
                P.dma("sp", V(pstg_ap, xstg.bufs),
                      V(p_d[l, st * T:(st + 1) * T, :].rearrange("(c p) f -> p c f", p=128), []))
                for pc in range(2):
                    ps = psf.next()
                    for tc in range(4):
                        P.tr(V(ps.t[:, 0, tc * 128:(tc + 1) * 128], ps.bufs), V(pstg_ap[:, tc, pc * 128:(pc + 1) * 128], xstg.bufs),
                             V(consts.t[:, 0, :], consts.bufs))
                    P.copy("act", V(pT.t[:, pc, :], pT.bufs), ps.c(0))
            for cb in range(2):
                wg = ws.get(l, "pleg", cb, 0, 8)
                wp = ws.get(l, "plep", cb, 0, 2)
                if ws.dry:
                    continue
                for m4 in range(4):
                    m = cb * 4 + m4
                    pg = psf.next()
                    proj(wg, m4, pg.c(0))
                    pe_ = psf.next()
                    for k in range(2):
                        P.mm(pe_.c(0), V(wp.t[:, k, m4 * 128:(m4 + 1) * 128], wp.bufs), V(pT.t[:, k, :], pT.bufs),
                             start=(k == 0), stop=(k == 1))
                    sg = f32r.next()
                    P.act(sg.c(0), pg.c(0), AF.Sigmoid)
                    P.tt("dve", sg.c(0), sg.c(0), pe_.c(0), ALU.mult)
                    P.tt("dve", resid.c(m), resid.c(m), sg.c(0), ALU.add)
                ws.release()

        def load_x(st):
            for tc in range(4):
                r0 = st * T + tc * 128
                P.dma("sp", V(xstg.t[:, 0, :], xstg.bufs), V(x_d[r0:r0 + 128, :], [xbuf_d]))
                for half in range(2):
                    ps = psf.next()
                    for c4 in range(4):
                        c = half * 4 + c4
                        P.tr(V(ps.t[:, 0, c4 * 128:(c4 + 1) * 128], ps.bufs), V(xstg.t[:, 0, c * 128:(c + 1) * 128], xstg.bufs),
                             V(consts.t[:, 0, :], consts.bufs))
                    P.copy("act", V(resid.t[:, half * 4:(half + 1) * 4, tc * 128:(tc + 1) * 128], resid.bufs[half * 4:(half + 1) * 4]),
                           V(ps.t[:, 0, :].rearrange("p (c t) -> p c t", c=4), ps.bufs))

        def store_out(st, normed):
            for tc in range(4):
                r0 = st * T + tc * 128
                for half in range(2):
                    ps = psf.next()
                    for c4 in range(4):
                        c = half * 4 + c4
                        P.tr(V(ps.t[:, 0, c4 * 128:(c4 + 1) * 128], ps.bufs), normed.c(c, slice(tc * 128, (tc + 1) * 128)),
                             V(consts.t[:, 0, :], consts.bufs))
                    P.copy("act", V(xstg.t[:, 0, half * 512:(half + 1) * 512], xstg.bufs), ps.c(0))
                P.dma("sp", V(out_d[r0:r0 + 128, :], [outbuf_d]), V(xstg.t[:, 0, :], xstg.bufs), sem_buf=xstg.bufs[0])

        def body():
            for st in range(NT):
                if not ws.dry:
                    load_x(st)
                for li, l in enumerate(layers):
                    def ca(k):
                        if st == 0 and not ws.dry:
                            conv_ahead(li * 4 + k + 2)
                    ca(0)
                    if "ffn1" in phases:
                        ffn(l, "ffn1", P_N1)
                    ca(1)
                    if "mixer" in phases:
                        mixer(l, st)
                    ca(2)
                    if "ffn2" in phases:
                        ffn(l, "ffn2", P_N2)
                    ca(3)
                    if "ple" in phases:
                        ple(l, st)
                if ws.dry:
                    continue
                if do_final:
                    rmsnorm(DEPTH * PL, out_f32=lambda c: resid.c(c))
                store_out(st, resid)

        body()
        ws.dry = False

        P.dma("sp", V(params.t[:, 0, :], params.bufs), V(par_d[:, :], []))
        P.dma("sp", V(consts.t[:], consts.bufs), V(cst_d[:, :, :], []))
        P.copy("dve", identb.c(0), V(consts.t[:, 0, :], consts.bufs))
        P.memset("dve", V(onesb.t[:, 0, :], onesb.bufs), 1.0 / 1024.0)
        P.memset("dve", V(onesb.t[:, 1, :], onesb.bufs), 1.0 / 512.0)
        P.copy("dve", cmb.c(0), V(consts.t[:, 3, :], consts.bufs))
        for l in layers:
            P.memset("dve", states[l].all(), 0.0)
            P.memset("dve", tailm[l].all(), 0.0)
            P.memset("dve", tails[l].all(), 0.0)
            pb = l * PL
            an_l = V(aneg.t[:, 0, l * NH:(l + 1) * NH], aneg.bufs)
            P.act(an_l, par(pb + P_AL, NH), AF.Exp)
            P.ts("dve", an_l, an_l, -1.0, None, ALU.mult)

        body()

        E = P.eng["sp"]
        b = xstg.bufs[0]
        E.h.wait_ge(b.sem, b.cnt)
        print(f"[build] S={S} layers={layers} inst={P.ninst} waits={P.nwaits} wreqs={len(ws.reqs)}")
    return nc


def _pack_params(inp):
    par = np.zeros((128, NPAR), np.float32)

    def fm(v, nchunk):
        return np.ascontiguousarray(np.asarray(v, np.float32).reshape(nchunk, 128).T)

    for l in range(DEPTH):
        pb = l * PL
        par[:, pb + P_N1:pb + P_N1 + 8] = fm(inp["ffn1_norm"][l], 8)
        par[:, pb + P_NM:pb + P_NM + 8] = fm(inp["mix_norm"][l], 8)
        par[:, pb + P_N2:pb + P_N2 + 8] = fm(inp["ffn2_norm"][l], 8)
        par[:, pb + P_NP:pb + P_NP + 8] = fm(inp["ple_norm"][l], 8)
        scw = np.asarray(inp["sc_conv_w"][l], np.float32)
        par[:, pb + P_SCW:pb + P_SCW + 24] = scw.reshape(3, 8, 128).transpose(2, 1, 0).reshape(128, 24)
        mcw = np.asarray(inp["m_conv_w"][l], np.float32)
        par[:, pb + P_MCW:pb + P_MCW + 96] = mcw.reshape(4, 24, 128).transpose(2, 1, 0).reshape(128, 96)
        par[:, pb + P_MCB:pb + P_MCB + 24] = fm(inp["m_conv_b"][l], 24)
        par[:, pb + P_MN:pb + P_MN + 16] = fm(inp["m_norm"][l], 16)
        par[:, pb + P_DTB:pb + P_DTB + NH] = np.asarray(inp["m_dt_bias"][l], np.float32)[None, :]
        par[:, pb + P_AL:pb + P_AL + NH] = np.asarray(inp["m_A_log"][l], np.float32)[None, :]
        par[:, pb + P_DD:pb + P_DD + NH] = np.asarray(inp["m_D"][l], np.float32)[None, :]
    par[:, DEPTH * PL:DEPTH * PL + 8] = fm(inp["final_norm"], 8)
    return par


def _consts():
    c = np.zeros((128, 5, 128), np.float32)
    i = np.arange(128)
    c[:, 0, :] = np.eye(128, dtype=np.float32)
    c[:, 1, :] = (i[:, None] <= i[None, :]).astype(np.float32)
    c[:, 2, :] = (i[:, None] > i[None, :]).astype(np.float32)
    c[:, 3, :] = (i[None, :] >= i[:, None]).astype(np.float32)
    c[:, 4, :] = 1.0
    return c


_CACHE = {}


def run_layers(h_in, inp, layers, do_final, S, ncores, trace=False, phases=("ffn1", "mixer", "ffn2", "ple")):
    key = (S, tuple(layers), do_final, tuple(phases))
    if key not in _CACHE:
        _CACHE[key] = build_program(S, list(layers), do_final, phases=phases)
    nc = _CACHE[key]
    par = _pack_params(inp)
    cst = _consts()
    wts = {n: np.ascontiguousarray(np.asarray(inp[n], np.float32)[list(layers)]) for n in WSHAPES}
    p_all = np.asarray(inp["p"], np.float32)
    in_maps = []
    for b in range(ncores):
        m = {"x": np.ascontiguousarray(h_in[b]), "p": np.ascontiguousarray(p_all[:, b, :S, :]), "params": par, "consts": cst}
        m.update(wts)
        in_maps.append(m)
    res = run_bass_kernel_spmd(nc, in_maps, core_ids=list(range(ncores)), **({"trace": True} if trace else {}))
    out = np.stack([np.asarray(r["out"], np.float32) for r in res.results], axis=0)
    return out, res


def kernel(**inputs):
    x = np.asarray(inputs["x"], np.float32)
    out, _ = run_layers(x, inputs, [0, 1, 2, 3], True, 4096, 8)
    return out
```

```python
import numpy as np
import os
MK_STOP = float(os.environ.get('MK_STOP', '99'))
from contextlib import ExitStack
import concourse.bass as bass
import concourse.mybir as mybir
from concourse.bass_utils import run_bass_kernel_spmd

F32 = mybir.dt.float32
BF16 = mybir.dt.bfloat16
AF = mybir.ActivationFunctionType
ALU = mybir.AluOpType

D = 1024
DFF = 2816
NFC = DFF // 128
PLE = 256
NH = 32
T = 512
EPS = 1e-6
DEPTH = 4
PROJ = 10272
O_SCB, O_SCC, O_SCX, O_Z, O_XS, O_B, O_C, O_DT, O_GA, O_GM = 0, 1024, 2048, 3072, 5120, 7168, 7680, 8192, 8224, 9248
PL = 288
P_N1, P_NM, P_N2, P_NP, P_SCW, P_MCW, P_MCB, P_MN, P_DTB, P_AL, P_DD = 0, 8, 16, 24, 32, 56, 152, 176, 192, 224, 256
NPAR = DEPTH * PL + 8

SAME_ENGINE_SYNC = True
CONV_INFLIGHT = 2


class Buf:
    __slots__ = ("w", "r", "sem", "cnt", "name", "psum")

    def __init__(self, name=""):
        self.psum = False
        self.w = None
        self.r = []
        self.sem = None
        self.cnt = 0
        self.name = name


class V:
    __slots__ = ("ap", "bufs")

    def __init__(self, ap, bufs):
        self.ap = ap
        self.bufs = bufs if isinstance(bufs, (list, tuple)) else [bufs]


class Tn:
    def __init__(self, t, nb=1, name=""):
        self.t = t
        self.bufs = [Buf(f"{name}{i}") for i in range(nb)]

    def c(self, i, sl=None):
        ap = self.t[:, i] if sl is None else self.t[:, i, sl]
        return V(ap, [self.bufs[i]])

    def all(self):
        return V(self.t[:], self.bufs)

    def v(self, ap, idx=None):
        return V(ap, self.bufs if idx is None else [self.bufs[i] for i in idx])


class GSet:
    def __init__(self, tn_main, off_xs, off_z, tn_bc, off_bc):
        self.m, self.ox, self.oz, self.bc, self.ob = tn_main, off_xs, off_z, tn_bc, off_bc

    def c(self, i, sl=None):
        if i < 4:
            return self.m.c(self.ox + i, sl)
        if i < 6:
            return self.bc.c(self.ob + i - 4, sl)
        return self.m.c(self.oz + i - 6, sl)

    def zview(self, tsl):
        return V(self.m.t[:, self.oz:self.oz + 4, tsl], self.m.bufs[self.oz:self.oz + 4])


class Eng:
    def __init__(self, name, h, sem):
        self.name = name
        self.h = h
        self.sem = sem
        self.cnt = 0
        self.seen = {}


class Prog:
    def __init__(self, nc, es):
        self.nc = nc
        self.es = es
        self.sems = {}
        self.emitted = {}
        self.eng = {}
        for name, h in (("pe", nc.tensor), ("act", nc.scalar), ("dve", nc.vector), ("pool", nc.gpsimd), ("sp", nc.sync)):
            sem = self.newsem("e_" + name)
            self.eng[name] = Eng(name, h, sem)
        self.nwaits = 0
        self.ninst = 0

    def newsem(self, name):
        s = self.es.enter_context(self.nc.semaphore(name))
        self.emitted[id(s)] = 0
        self.sems[id(s)] = s
        return s

    def _wait(self, E, deps):
        best = {}
        for tok in deps:
            if tok is None:
                continue
            sem, val = tok
            k = id(sem)
            if best.get(k, (None, 0))[1] < val:
                best[k] = (sem, val)
        for k, (sem, val) in best.items():
            if sem is E.sem and (E.name in ("pe", "sp") or not SAME_ENGINE_SYNC):
                continue
            if E.seen.get(k, 0) >= val:
                continue
            assert self.emitted[k] >= val, f"wait on unemitted token {E.name} {val} > {self.emitted[k]}"
            E.h.wait_ge(sem, val)
            E.seen[k] = val
            self.nwaits += 1

    def emit(self, eng, fn, reads, writes, signal=True):
        E = self.eng[eng]
        deps = []
        for b in reads:
            deps.append(b.w)
            if b.psum:
                deps.extend(t for t in b.r if t[0] is not E.sem)
        for b in writes:
            deps.append(b.w)
            deps.extend(b.r)
        self._wait(E, deps)
        ins = fn(E.h)
        self.ninst += 1
        tok = (E.sem, E.cnt + 1)
        if signal:
            ins.then_inc(E.sem, 1)
            E.cnt += 1
            self.emitted[id(E.sem)] = E.cnt
        for b in reads:
            b.r.append(tok)
        for b in writes:
            b.w = tok
            b.r = []
        return ins

    def dma(self, q, out, in_, sem_buf=None, nodep=False):
        E = self.eng[q]
        deps = []
        for b in in_.bufs:
            deps.append(b.w)
        for b in out.bufs:
            deps.append(b.w)
            deps.extend(b.r)
        if not nodep:
            self._wait(E, deps)
        sb = sem_buf if sem_buf is not None else out.bufs[0]
        if sb.sem is None:
            sb.sem = self.newsem("d_" + sb.name)
        ins = E.h.dma_start(out=out.ap, in_=in_.ap)
        ins.then_inc(sb.sem, 16)
        sb.cnt += 16
        self.emitted[id(sb.sem)] = sb.cnt
        tok = (sb.sem, sb.cnt)
        for b in in_.bufs:
            b.r.append(tok)
        for b in out.bufs:
            b.w = tok
            b.r = []
        self.ninst += 1
        return tok

    def mm(self, out, lhsT, rhs, start=True, stop=True, signal=True):
        return self.emit("pe", lambda e: e.matmul(out.ap, lhsT=lhsT.ap, rhs=rhs.ap, start=start, stop=stop),
                         lhsT.bufs + rhs.bufs, out.bufs, signal)

    def tr(self, out, in_, ident, signal=True):
        return self.emit("pe", lambda e: e.transpose(out.ap, in_.ap, ident.ap), in_.bufs + ident.bufs, out.bufs, signal)

    def act(self, out, in_, func, bias=None, scale=1.0):
        rd = list(in_.bufs)
        kw = {}
        if bias is not None:
            if isinstance(bias, V):
                rd += bias.bufs
                kw["bias"] = bias.ap
            else:
                kw["bias"] = float(bias)
        if isinstance(scale, V):
            rd += scale.bufs
            kw["scale"] = scale.ap
        else:
            kw["scale"] = float(scale)
        return self.emit("act", lambda e: e.activation(out=out.ap, in_=in_.ap, func=func, **kw), rd, out.bufs)

    def tt(self, eng, out, in0, in1, op):
        return self.emit(eng, lambda e: e.tensor_tensor(out=out.ap, in0=in0.ap, in1=in1.ap, op=op),
                         in0.bufs + in1.bufs, out.bufs)

    def ts(self, eng, out, in0, s1, s2, op0, op1=None):
        rd = list(in0.bufs)
        a1 = s1
        a2 = s2
        if isinstance(s1, V):
            rd += s1.bufs
            a1 = s1.ap
        if isinstance(s2, V):
            rd += s2.bufs
            a2 = s2.ap
        if op1 is None:
            return self.emit(eng, lambda e: e.tensor_scalar(out=out.ap, in0=in0.ap, scalar1=a1, scalar2=None, op0=op0), rd, out.bufs)
        return self.emit(eng, lambda e: e.tensor_scalar(out=out.ap, in0=in0.ap, scalar1=a1, scalar2=a2, op0=op0, op1=op1), rd, out.bufs)

    def stt(self, out, in0, scalar, in1, op0, op1):
        rd = in0.bufs + in1.bufs
        a = scalar
        if isinstance(scalar, V):
            rd = rd + scalar.bufs
            a = scalar.ap
        return self.emit("dve", lambda e: e.scalar_tensor_tensor(out=out.ap, in0=in0.ap, scalar=a, in1=in1.ap, op0=op0, op1=op1),
                         rd, out.bufs)

    def copy(self, eng, out, in_):
        if eng == "act":
            return self.emit("act", lambda e: e.activation(out=out.ap, in_=in_.ap, func=AF.Copy), in_.bufs, out.bufs)
        return self.emit(eng, lambda e: e.tensor_copy(out=out.ap, in_=in_.ap), in_.bufs, out.bufs)

    def recip(self, out, in_):
        return self.emit("dve", lambda e: e.reciprocal(out=out.ap, in_=in_.ap), in_.bufs, out.bufs)

    def memset(self, eng, out, val):
        return self.emit(eng, lambda e: e.memset(out.ap, val), [], out.bufs)


class Ring:
    def __init__(self, items):
        self.items = items
        self.i = 0

    def next(self):
        x = self.items[self.i % len(self.items)]
        self.i += 1
        return x


def _blocks(c0, n, step=512):
    out = []
    c = 0
    while c < n:
        out.append((c0 + c, min(step, n - c)))
        c += step
    return out


WSPEC = {}
for _f in ("ffn1", "ffn2"):
    WSPEC[_f + "_wg"] = (_f + "_wg", D, _blocks(0, DFF))
    WSPEC[_f + "_wu"] = (_f + "_wu", D, _blocks(0, DFF))
    WSPEC[_f + "_wd"] = (_f + "_wd", DFF, _blocks(0, D))
WSPEC["dt"] = ("w_in", D, [(O_DT, 32)])
WSPEC["xs"] = ("w_in", D, _blocks(O_XS, 2048))
WSPEC["B"] = ("w_in", D, _blocks(O_B, 512, 128))
WSPEC["C"] = ("w_in", D, _blocks(O_C, 512, 128))
WSPEC["z"] = ("w_in", D, _blocks(O_Z, 2048))
WSPEC["scb"] = ("w_in", D, _blocks(O_SCB, 1024))
WSPEC["scc"] = ("w_in", D, _blocks(O_SCC, 1024))
WSPEC["scx"] = ("w_in", D, _blocks(O_SCX, 1024))
WSPEC["ga"] = ("w_in", D, _blocks(O_GA, 1024))
WSPEC["gm"] = ("w_in", D, _blocks(O_GM, 1024))
WSPEC["scwo"] = ("sc_w_out", D, _blocks(0, D))
WSPEC["mwo"] = ("m_w_out", 2048, _blocks(0, D))
WSPEC["wo"] = ("w_o", D, _blocks(0, D))
WSPEC["pleg"] = ("ple_w_gate", D, _blocks(0, D))
WSPEC["plep"] = ("ple_w_proj", PLE, _blocks(0, D))

WSHAPES = {"ffn1_wg": (D, DFF), "ffn1_wu": (D, DFF), "ffn1_wd": (DFF, D), "ffn2_wg": (D, DFF), "ffn2_wu": (D, DFF),
           "ffn2_wd": (DFF, D), "w_in": (D, PROJ), "sc_w_out": (D, D), "m_w_out": (2048, D), "w_o": (D, D),
           "ple_w_gate": (D, D), "ple_w_proj": (PLE, D)}
WORDER = ["ffn1_wg", "ffn1_wu", "ffn1_wd", "dt", "xs", "B", "C", "z", "scc", "scx", "scb", "scwo", "mwo", "ga", "gm", "wo",
          "ffn2_wg", "ffn2_wu", "ffn2_wd", "pleg", "plep"]


def build_program(S, layers, do_final, nslab=5, phases=("ffn1", "mixer", "ffn2", "ple")):
    NT = S // T
    nc = bass.Bass("TRN2", target_bir_lowering=False)
    es = ExitStack()
    with es:
        P = Prog(nc, es)

        def dram(name, shape, dt, kind):
            return nc.dram_tensor(name, list(shape), dt, kind=kind).ap()

        def sb(name, shape, dt):
            return es.enter_context(nc.sbuf_tensor(name, list(shape), dt))

        def pp(name, shape, dt):
            return es.enter_context(nc.psum_tensor(name, list(shape), dt))

        x_d = dram("x", (S, D), F32, "ExternalInput")
        p_d = dram("p", (DEPTH, S, PLE), F32, "ExternalInput")
        par_d = dram("params", (128, NPAR), F32, "ExternalInput")
        cst_d = dram("consts", (128, 5, 128), F32, "ExternalInput")
        out_d = dram("out", (S, D), F32, "ExternalOutput")
        w_d = {n: dram(n, (len(layers),) + shp, F32, "ExternalInput") for n, shp in WSHAPES.items()}
        xbuf_d = Buf("x_d")
        outbuf_d = Buf("out_d")

        scr = {}
        for l in layers:
            for name in WORDER:
                src, K, blks = WSPEC[name]
                kc = K // 128
                t = dram(f"scr_{l}_{name}", (len(blks), 128, kc, 512), BF16, "Internal")
                scr[(l, name)] = (t, Buf(f"scr{l}{name}"), kc, blks)

        resid = Tn(sb("resid", (128, 8, T), F32), 8, "res")
        xn = Tn(sb("xn", (128, 8, T), BF16), 8, "xn")
        actp = Tn(sb("actp", (128, 24, T), BF16), 24, "act")
        grpA = Tn(sb("grpA", (128, 10, T), BF16), 10, "grpA")
        bc2 = Tn(sb("bc2", (128, 2, T), BF16), 2, "bc2")
        pT = bc2
        gy = Tn(sb("gy", (128, 4, T), F32), 4, "gy")
        states = {l: Tn(sb(f"st{l}", (128, 4, T), F32), 4, f"st{l}") for l in layers}
        sbf = Tn(sb("sbf", (128, 4, T), BF16), 4, "sbf")
        tailm = {l: Tn(sb(f"tm{l}", (128, 24, 3), F32), 24, f"tm{l}") for l in layers}
        tails = {l: Tn(sb(f"ts{l}", (128, 8, 2), F32), 8, f"ts{l}") for l in layers}
        cvr = Ring([Tn(sb(f"cv{i}", (128, 1, T + 3), F32), 1, f"cv{i}") for i in range(2)])
        f32r = Ring([Tn(sb(f"f32r{i}", (128, 1, T), F32), 1, f"f32r{i}") for i in range(4)])
        rstd = Tn(sb("rstd", (128, 1, T), F32), 1, "rstd")
        slabs = [Tn(sb(f"slab{i}", (128, 8, 512), BF16), 1, f"slab{i}") for i in range(nslab)]
        params = Tn(sb("params_sb", (128, 1, NPAR), F32), 1, "params")
        consts = Tn(sb("consts_sb", (128, 5, 128), F32), 1, "consts")
        identb = Tn(sb("identb", (128, 1, 128), BF16), 1, "identb")
        onesb = Tn(sb("onesb", (128, 2, 128), BF16), 1, "onesb")
        cmb = Tn(sb("cmb", (128, 1, 128), F32), 1, "cmb")
        aneg = Tn(sb("aneg", (128, 1, DEPTH * NH), F32), 1, "aneg")
        xstg = Tn(sb("xstg", (128, 1, D), F32), 1, "xstg")
        dtt = {n: Tn(sb("dt_" + n, (128, 4, NH), F32), 1, "dt_" + n) for n in
               ("t", "ab", "ex", "dt", "a", "acum", "E", "dif", "dte", "cd", "dtdte")}
        Xb = Tn(sb("Xb", (128, 4, T), BF16), 4, "Xb")
        Xdb = Tn(sb("Xdb", (128, 2, T), BF16), 2, "Xdb")
        xsD = Tn(sb("xsD", (128, 4, T), BF16), 4, "xsD")
        BTb = Tn(sb("BTb", (128, 2, 128), BF16), 2, "BTb")
        CBm = Tn(sb("CBm", (128, 2, 128), BF16), 2, "CBm")
        Lfr = Ring([Tn(sb(f"Lf{i}", (128, 8, 128), BF16), 1, f"Lf{i}") for i in range(2)])
        triUb = Tn(sb("triUb", (128, 1, 128), BF16), 1, "triUb")
        dec = Tn(sb("dec", (128, 4, 1024), BF16), 4, "dec")
        Mh = dec
        ybf = Tn(sb("ybf", (128, 4, T), BF16), 4, "ybf")
        print("[build] sbuf bytes remaining", nc.sbuf_bytes_remaining)

        ps6 = [Tn(pp(f"psf{i}", (128, 1, 512), F32), 1, f"psf{i}") for i in range(6)]
        psf = Ring(ps6)
        pss = Ring(ps6[:4])
        psp = Ring(ps6[4:])
        psb = Ring([Tn(pp(f"psb{i}", (128, 1, 1024), BF16), 1, f"psb{i}") for i in range(2)])
        for tn_ in psf.items + psb.items:
            tn_.bufs[0].psum = True

        conv_hist = []
        PH_NAMES = {"ffn1": ["ffn1_wg", "ffn1_wu", "ffn1_wd"],
                    "mixer": ["dt", "xs", "B", "C", "z", "scc", "scx", "scb", "scwo", "mwo", "ga", "gm", "wo"],
                    "ffn2": ["ffn2_wg", "ffn2_wu", "ffn2_wd"], "ple": ["pleg", "plep"]}
        conv_seq = [(l, ph) for l in layers for ph in ("ffn1", "mixer", "ffn2", "ple")]
        conv_state = {"pos": 0}

        def conv_ahead(n):
            while conv_state["pos"] < min(len(conv_seq), n):
                l, ph = conv_seq[conv_state["pos"]]
                conv_state["pos"] += 1
                for name in PH_NAMES[ph]:
                    if name not in WORDER:
                        continue
                    src, K, blks = WSPEC[name]
                    t, buf, kc, _ = scr[(l, name)]
                    if len(conv_hist) >= CONV_INFLIGHT:
                        pb_ = conv_hist[-CONV_INFLIGHT]
                        P.eng["pool"].h.wait_ge(pb_.sem, pb_.cnt)
                    conv_hist.append(buf)
                    for bi, (c0, ncol) in enumerate(blks):
                        for k0 in range(0, kc, 8):
                            k1 = min(kc, k0 + 8)
                            src_ap = w_d[src][layers.index(l), k0 * 128:k1 * 128, c0:c0 + ncol].rearrange("(k p) c -> p k c", p=128)
                            P.dma("pool", V(t[bi, :, k0:k1, 0:ncol], [buf]), V(src_ap, []), nodep=True)

        class WS:
            def __init__(self):
                self.reqs = []
                self.dry = True
                self.i = 0
                self.loaded = 0
                self.released = 0

            def get(self, l, name, bi, k0, k1):
                if self.dry:
                    self.reqs.append((l, name, bi, k0, k1))
                    return None
                i = self.i
                assert self.reqs[i] == (l, name, bi, k0, k1)
                assert i - nslab < self.released, "too many slabs held"
                self._pump()
                assert self.loaded > i
                self.i += 1
                return slabs[i % nslab]

            def _pump(self):
                while self.loaded < min(len(self.reqs), self.released + nslab):
                    self._load(self.loaded)
                    self.loaded += 1

            def release(self):
                if self.dry:
                    return
                self.released = self.i
                self._pump()

            def _load(self, j):
                l, name, bi, k0, k1 = self.reqs[j]
                t, buf, kc, blks = scr[(l, name)]
                ncol = blks[bi][1]
                sl = slabs[j % nslab]
                P.dma("sp", V(sl.t[:, 0:k1 - k0, 0:ncol], sl.bufs), V(t[bi, :, k0:k1, 0:ncol], [buf]))

        ws = WS()

        def par(col, n=1):
            return V(params.t[:, 0, col:col + n], params.bufs)

        stats = {"ps": None, "n": 0}

        def stats_sq(chunks):
            for c in chunks:
                P.act(xn.c(c), resid.c(c), AF.Square)

        def stats_mm(chunks):
            for c in chunks:
                if stats["n"] == 0:
                    stats["ps"] = psf.next()
                P.mm(stats["ps"].c(0), V(onesb.t[:, 0, :], onesb.bufs), xn.c(c), start=(stats["n"] == 0), stop=(stats["n"] == 7))
                stats["n"] += 1

        def rmsnorm(l_col, out_f32=None):
            if ws.dry:
                return
            if stats["n"] == 0:
                stats_sq(range(8))
                stats_mm(range(8))
            assert stats["n"] == 8
            ps = stats["ps"]
            stats["n"] = 0
            stats["ps"] = None
            rstd_from(ps)
            for c in range(8):
                o = xn.c(c) if out_f32 is None else out_f32(c)
                P.stt(o, resid.c(c), par(l_col + c), rstd.c(0), ALU.mult, ALU.mult)

        def ffn(l, pre, ncol_norm):
            rmsnorm(l * PL + ncol_norm)
            blks = WSPEC[pre + "_wg"][2]
            for bi, (c0, ncol) in enumerate(blks):
                wg = ws.get(l, pre + "_wg", bi, 0, 8)
                wu = ws.get(l, pre + "_wu", bi, 0, 8)
                if ws.dry:
                    continue
                for fc in range(ncol // 128):
                    j = bi * 4 + fc
                    pg = psf.next()
                    for k in range(8):
                        P.mm(pg.c(0), V(wg.t[:, k, fc * 128:(fc + 1) * 128], wg.bufs), xn.c(k), start=(k == 0), stop=(k == 7))
                    pu = psf.next()
                    for k in range(8):
                        P.mm(pu.c(0), V(wu.t[:, k, fc * 128:(fc + 1) * 128], wu.bufs), xn.c(k), start=(k == 0), stop=(k == 7))
                    sg = f32r.next()
                    P.act(sg.c(0), pg.c(0), AF.Silu)
                    P.tt("dve", actp.c(j), sg.c(0), pu.c(0), ALU.mult)
                ws.release()
            for cb in range(2):
                pss = [psf.next() for _ in range(4)] if not ws.dry else None
                for k0 in range(0, NFC, 8):
                    k1 = min(NFC, k0 + 8)
                    wd = ws.get(l, pre + "_wd", cb, k0, k1)
                    if ws.dry:
                        continue
                    for kk in range(k1 - k0):
                        j = k0 + kk
                        for m4 in range(4):
                            P.mm(pss[m4].c(0), V(wd.t[:, kk, m4 * 128:(m4 + 1) * 128], wd.bufs), actp.c(j),
                                 start=(j == 0), stop=(j == NFC - 1))
                    ws.release()
                if ws.dry:
                    continue
                if cb == 1:
                    stats_mm(range(0, 4))
                for m4 in range(4):
                    m = cb * 4 + m4
                    P.stt(resid.c(m), pss[m4].c(0), 0.5, resid.c(m), ALU.mult, ALU.add)
                stats_sq(range(cb * 4, cb * 4 + 4))
                if cb == 1:
                    stats_mm(range(4, 8))

        def proj(slab, fc, out_ps):
            for k in range(8):
                P.mm(out_ps, V(slab.t[:, k, fc * 128:(fc + 1) * 128], slab.bufs), xn.c(k), start=(k == 0), stop=(k == 7))

        def bc_h(tn, tc, g):
            return V(tn.t[:, tc, g * 8:(g + 1) * 8].unsqueeze(2).to_broadcast([128, 8, 64]), tn.bufs)

        def rstd_from(ps):
            s_ = f32r.next()
            P.act(s_.c(0), ps.c(0), AF.Ln, bias=EPS)
            P.act(rstd.c(0), s_.c(0), AF.Exp, scale=-0.5)

        def proj_get(l, g):
            return (ws.get(l, "xs", g, 0, 8), ws.get(l, "B", g, 0, 8), ws.get(l, "C", g, 0, 8), ws.get(l, "z", g, 0, 8))

        def proj_units(l, g, G, sl4):
            pb = l * PL
            wxs, wB, wC, wz = sl4
            pend = []
            for i in range(6):
                if i < 4:
                    slab, fc, ch = wxs, i, g * 4 + i
                elif i == 4:
                    slab, fc, ch = wB, 0, 16 + g
                else:
                    slab, fc, ch = wC, 0, 20 + g
                ps = psp.next()
                proj(slab, fc, ps.c(0))
                cv = cvr.next()
                tl = tailm[l]
                wc = pb + P_MCW + ch * 4
                acc = f32r.next()
                P.act(acc.c(0), ps.c(0), AF.Identity, bias=par(pb + P_MCB + ch), scale=par(wc + 3))
                P.copy("pool", V(cv.t[:, 0, 0:3], cv.bufs), V(tl.t[:, ch, :], [tl.bufs[ch]]))
                P.copy("act", V(cv.t[:, 0, 3:3 + T], cv.bufs), ps.c(0))
                P.copy("pool", V(tl.t[:, ch, :], [tl.bufs[ch]]), V(cv.t[:, 0, T:T + 3], cv.bufs))
                p2 = f32r.next()
                P.act(p2.c(0), V(cv.t[:, 0, 2:2 + T], cv.bufs), AF.Identity, scale=par(wc + 2))
                for (o_, a_) in pend:
                    P.act(o_, a_, AF.Silu)
                pend = [(G.c(i), acc.c(0))]
                P.tt("pool", acc.c(0), acc.c(0), p2.c(0), ALU.add)
                for k in range(0, 2):
                    P.stt(acc.c(0), V(cv.t[:, 0, k:k + T], cv.bufs), par(wc + k), acc.c(0), ALU.mult, ALU.add)
                yield
            for fc in range(4):
                ps = psp.next()
                proj(wz, fc, ps.c(0))
                for (o_, a_) in pend:
                    P.act(o_, a_, AF.Silu)
                pend = []
                P.act(G.c(6 + fc), ps.c(0), AF.Silu)
                if fc == 3:
                    ws.release()
                yield

        def mixer(l, st):
            pb = l * PL
            rmsnorm(pb + P_NM)
            d = dtt
            v4 = lambda n: V(d[n].t[:], d[n].bufs)
            gsets = [GSet(grpA, 0, 6, grpA, 4), GSet(actp, 16, 20, bc2, 0)]
            wdt = ws.get(l, "dt", 0, 0, 8)
            if not ws.dry:
                ps = psf.next()
                for tc in range(4):
                    for k in range(8):
                        P.mm(V(ps.t[:, 0, tc * 32:(tc + 1) * 32], ps.bufs), V(xn.t[:, k, tc * 128:(tc + 1) * 128], [xn.bufs[k]]),
                             V(wdt.t[:, k, 0:32], wdt.bufs), start=(k == 0), stop=(k == 7))
                ps4 = V(ps.t[:, 0, 0:128].rearrange("p (c h) -> p c h", c=4), ps.bufs)
                bias_b = V(params.t[:, 0, pb + P_DTB:pb + P_DTB + NH].unsqueeze(1).to_broadcast([128, 4, NH]), params.bufs)
                P.tt("dve", v4("t"), ps4, bias_b, ALU.add)
                ws.release()
                P.act(v4("ab"), v4("t"), AF.Abs)
                P.act(v4("ex"), v4("ab"), AF.Exp, scale=-1.0)
                P.act(v4("ab"), v4("ex"), AF.Ln, bias=1.0)
                P.stt(v4("dt"), v4("t"), 0.0, v4("ab"), ALU.max, ALU.add)
                an_b = V(aneg.t[:, 0, l * NH:(l + 1) * NH].unsqueeze(1).to_broadcast([128, 4, NH]), aneg.bufs)
                P.tt("dve", v4("a"), v4("dt"), an_b, ALU.mult)
            sl4 = proj_get(l, 0)
            if not ws.dry:
                for _ in proj_units(l, 0, gsets[0], sl4):
                    pass
                pa = psf.next()
                pt = psf.next()
                for tc in range(4):
                    P.mm(V(pa.t[:, 0, tc * 32:(tc + 1) * 32], pa.bufs), V(consts.t[:, 1, :], consts.bufs),
                         V(d["a"].t[:, tc, :], d["a"].bufs))
                    P.mm(V(pt.t[:, 0, tc * 32:(tc + 1) * 32], pt.bufs), V(consts.t[:, 4, :], consts.bufs),
                         V(d["a"].t[:, tc, :], d["a"].bufs))
                pa4 = V(pa.t[:, 0, 0:128].rearrange("p (c h) -> p c h", c=4), pa.bufs)
                pt4 = V(pt.t[:, 0, 0:128].rearrange("p (c h) -> p c h", c=4), pt.bufs)
                P.copy("dve", v4("acum"), pa4)
                P.act(v4("E"), v4("acum"), AF.Exp)
                P.tt("dve", v4("dif"), pt4, v4("acum"), ALU.subtract)
                P.act(v4("dte"), v4("dif"), AF.Exp)
                P.tt("dve", v4("ex"), v4("dif"), v4("acum"), ALU.add)
                P.act(v4("cd"), v4("ex"), AF.Exp)
                P.tt("dve", v4("dtdte"), v4("dt"), v4("dte"), ALU.mult)

            for g in range(4):
                G = gsets[g % 2]
                gen = None
                if g < 3:
                    sl4 = proj_get(l, g + 1)
                    if not ws.dry:
                        gen = proj_units(l, g + 1, gsets[(g + 1) % 2], sl4)
                if ws.dry:
                    continue

                def pump(n=1):
                    if gen is not None:
                        for _ in range(n):
                            next(gen, None)

                S_ = states[l]
                v3 = lambda tn, i: V(tn.t[:, i, :].rearrange("p (h d) -> p h d", h=8), [tn.bufs[i]])
                P.copy("act", sbf.c(0), S_.c(g))
                Lfs = {}
                pxss = {}

                def t_evac(tc):
                    tsl = slice(tc * 128, (tc + 1) * 128)
                    pxs = psb.next()
                    for fc in range(4):
                        P.tr(V(pxs.t[:, 0, fc * 128:(fc + 1) * 128], pxs.bufs), G.c(fc, tsl), identb.c(0))
                    P.tr(V(pxs.t[:, 0, 512:640], pxs.bufs), G.c(4, tsl), identb.c(0))
                    pcb = pss.next()
                    P.mm(V(pcb.t[:, 0, 0:128], pcb.bufs), G.c(4, tsl), G.c(5, tsl))
                    Lf = Lfr.next()
                    Lfs[tc] = Lf
                    ms_b = V(consts.t[:, 2, :].unsqueeze(1).to_broadcast([128, 8, 128]), consts.bufs)
                    a_b = V(d["a"].t[:, tc, g * 8:(g + 1) * 8].unsqueeze(2).to_broadcast([128, 8, 128]), d["a"].bufs)
                    P.tt("pool", V(Lf.t[:], Lf.bufs), ms_b, a_b, ALU.mult)
                    pxs3 = V(pxs.t[:, 0, 0:T].rearrange("p (h d) -> p h d", h=8), pxs.bufs)
                    P.tt("dve", v3(Xb, tc), pxs3, bc_h(d["dt"], tc, g), ALU.mult)
                    P.tt("dve", v3(Xdb, tc % 2), pxs3, bc_h(d["dtdte"], tc, g), ALU.mult)
                    dd_b = V(params.t[:, 0, pb + P_DD + g * 8:pb + P_DD + (g + 1) * 8].unsqueeze(2).to_broadcast([128, 8, 64]), params.bufs)
                    P.tt("dve", v3(xsD, tc), pxs3, dd_b, ALU.mult)
                    P.copy("dve", BTb.c(tc % 2), V(pxs.t[:, 0, 512:640], pxs.bufs))
                    P.tt("dve", CBm.c(tc % 2), V(pcb.t[:, 0, 0:128], pcb.bufs), cmb.c(0), ALU.mult)

                def decay(tc):
                    Lf = Lfs[tc]
                    for hb in range(2):
                        pd = pss.next()
                        for hh in range(4):
                            h = hb * 4 + hh
                            P.mm(V(pd.t[:, 0, hh * 128:(hh + 1) * 128], pd.bufs), V(Lf.t[:, h, :], Lf.bufs),
                                 triUb.c(0))
                        P.act(V(dec.t[:, tc, hb * 512:(hb + 1) * 512].rearrange("p (h s) -> p h s", h=4), [dec.bufs[tc]]),
                              V(pd.t[:, 0, :].rearrange("p (h s) -> p h s", h=4), pd.bufs), AF.Exp)
                    cb_b = V(CBm.t[:, tc % 2, :].unsqueeze(1).to_broadcast([128, 8, 128]), [CBm.bufs[tc % 2]])
                    P.tt("pool", V(Mh.t[:, tc, :].rearrange("p (h s) -> p h s", h=8), [Mh.bufs[tc]]),
                         V(dec.t[:, tc, :].rearrange("p (h s) -> p h s", h=8), [dec.bufs[tc]]), cb_b, ALU.mult)

                def state(tc):
                    pst = pss.next()
                    P.mm(pst.c(0), BTb.c(tc % 2), Xdb.c(tc % 2))
                    s3 = V(S_.t[:, g, :].rearrange("p (h d) -> p h d", h=8), [S_.bufs[g]])
                    P.tt("pool", s3, s3, bc_h(d["cd"], tc, g), ALU.mult)
                    P.tt("dve", S_.c(g), S_.c(g), pst.c(0), ALU.add)
                    if tc < 3:
                        P.copy("act", sbf.c(tc + 1), S_.c(g))

                def ycomb(tc):
                    tsl = slice(tc * 128, (tc + 1) * 128)
                    py = pss.next()
                    for h in range(8):
                        P.mm(V(py.t[:, 0, h * 64:(h + 1) * 64], py.bufs), V(Mh.t[:, tc, h * 128:(h + 1) * 128], [Mh.bufs[tc]]),
                             V(Xb.t[:, tc, h * 64:(h + 1) * 64], [Xb.bufs[tc]]))
                    pyo = pss.next()
                    P.mm(pyo.c(0), G.c(5, tsl), sbf.c(tc))
                    t1 = f32r.next()
                    pyo3 = V(pyo.t[:, 0, :].rearrange("p (h d) -> p h d", h=8), pyo.bufs)
                    P.tt("dve", V(t1.t[:, 0, :].rearrange("p (h d) -> p h d", h=8), t1.bufs), pyo3, bc_h(d["E"], tc, g), ALU.mult)
                    P.tt("dve", t1.c(0), t1.c(0), py.c(0), ALU.add)
                    P.tt("pool", ybf.c(tc), t1.c(0), xsD.c(tc), ALU.add)

                def ygate(tc):
                    tsl = slice(tc * 128, (tc + 1) * 128)
                    pyt = psb.next()
                    for fc in range(4):
                        P.tr(V(pyt.t[:, 0, fc * 128:(fc + 1) * 128], pyt.bufs), V(ybf.t[:, tc, fc * 128:(fc + 1) * 128], [ybf.bufs[tc]]),
                             identb.c(0))
                    P.tt("dve", V(gy.t[:, :, tsl], gy.bufs), V(pyt.t[:, 0, 0:T].rearrange("p (f t) -> p f t", f=4), pyt.bufs),
                         G.zview(tsl), ALU.mult)

                t_evac(0)
                for tc in range(4):
                    if tc < 3:
                        t_evac(tc + 1)
                    decay(tc)
                    state(tc)
                    pump(2)
                for tc in range(4):
                    ycomb(tc)
                    if tc >= 1:
                        ygate(tc - 1)
                    pump(1)
                pump(10)
                ygate(3)
                ps = pss.next()
                for fc in range(4):
                    P.act(actp.c(g * 4 + fc), gy.c(fc), AF.Square)
                    P.mm(ps.c(0), V(onesb.t[:, 1, :], onesb.bufs), actp.c(g * 4 + fc), start=(fc == 0), stop=(fc == 3))
                rstd_from(ps)
                for fc in range(4):
                    P.stt(actp.c(g * 4 + fc), gy.c(fc), par(pb + P_MN + g * 4 + fc), rstd.c(0), ALU.mult, ALU.mult)

            for cb in range(2):
                wc_ = ws.get(l, "scc", cb, 0, 8)
                wx_ = ws.get(l, "scx", cb, 0, 8)
                wb_ = ws.get(l, "scb", cb, 0, 8)
                if ws.dry:
                    continue
                for fc in range(4):
                    ch = cb * 4 + fc
                    pc = psf.next()
                    proj(wc_, fc, pc.c(0))
                    px = psf.next()
                    proj(wx_, fc, px.c(0))
                    pbb = psf.next()
                    proj(wb_, fc, pbb.c(0))
                    csb = f32r.next()
                    P.copy("act", csb.c(0), pc.c(0))
                    cv = cvr.next()
                    tl = tails[l]
                    P.copy("pool", V(cv.t[:, 0, 0:2], cv.bufs), V(tl.t[:, ch, :], [tl.bufs[ch]]))
                    P.tt("dve", V(cv.t[:, 0, 2:2 + T], cv.bufs), csb.c(0), px.c(0), ALU.mult)
                    P.copy("pool", V(tl.t[:, ch, :], [tl.bufs[ch]]), V(cv.t[:, 0, T:T + 2], cv.bufs))
                    acc = f32r.next()
                    wc = pb + P_SCW + ch * 3
                    P.act(acc.c(0), V(cv.t[:, 0, 0:T], cv.bufs), AF.Identity, scale=par(wc))
                    for k in range(1, 3):
                        P.stt(acc.c(0), V(cv.t[:, 0, k:k + T], cv.bufs), par(wc + k), acc.c(0), ALU.mult, ALU.add)
                    P.tt("dve", actp.c(16 + ch), acc.c(0), pbb.c(0), ALU.mult)
                ws.release()

            for cb in range(2):
                wsc = ws.get(l, "scwo", cb, 0, 8)
                wga = ws.get(l, "ga", cb, 0, 8)
                if not ws.dry:
                    for m4 in range(4):
                        m = cb * 4 + m4
                        msl = slice(m4 * 128, (m4 + 1) * 128)
                        pga = psf.next()
                        proj(wga, m4, pga.c(0))
                        pya = psf.next()
                        for k in range(8):
                            P.mm(pya.c(0), V(wsc.t[:, k, msl], wsc.bufs), actp.c(16 + k), start=(k == 0), stop=(k == 7))
                        sa = f32r.next()
                        P.act(sa.c(0), pga.c(0), AF.Sigmoid)
                        P.tt("dve", grpA.c(m), sa.c(0), pya.c(0), ALU.mult)
                    ws.release()
                wm0 = ws.get(l, "mwo", cb, 0, 8)
                wm1 = ws.get(l, "mwo", cb, 8, 16)
                wgm = ws.get(l, "gm", cb, 0, 8)
                if ws.dry:
                    continue
                for m4 in range(4):
                    m = cb * 4 + m4
                    msl = slice(m4 * 128, (m4 + 1) * 128)
                    pgm = psf.next()
                    proj(wgm, m4, pgm.c(0))
                    pym = psf.next()
                    for k in range(16):
                        wm = wm0 if k < 8 else wm1
                        P.mm(pym.c(0), V(wm.t[:, k % 8, msl], wm.bufs), actp.c(k), start=(k == 0), stop=(k == 15))
                    sm = f32r.next()
                    P.act(sm.c(0), pgm.c(0), AF.Sigmoid)
                    P.tt("dve", sm.c(0), sm.c(0), pym.c(0), ALU.mult)
                    P.tt("pool", grpA.c(m), grpA.c(m), sm.c(0), ALU.add)
                ws.release()
            for cb in range(2):
                wo = ws.get(l, "wo", cb, 0, 8)
                if ws.dry:
                    continue
                for m4 in range(4):
                    m = cb * 4 + m4
                    po = psf.next()
                    for k in range(8):
                        P.mm(po.c(0), V(wo.t[:, k, m4 * 128:(m4 + 1) * 128], wo.bufs), grpA.c(k), start=(k == 0), stop=(k == 7))
                    P.tt("dve", resid.c(m), po.c(0), resid.c(m), ALU.add)
                ws.release()
                if cb == 1:
                    stats_mm(range(0, 4))
                stats_sq(range(cb * 4, cb * 4 + 4))
                if cb == 1:
                    stats_mm(range(4, 8))

        def ple(l, st):
            pb = l * PL
            rmsnorm(pb + P_NP)
            if not ws.dry:
                pstg_ap = xstg.t[:, 0, :].rearrange("p (c f) -> p c f", c=4)
                P.dma("sp", V(pstg_ap, xstg.bufs),
                      V(p_d[l, st * T:(st + 1) * T, :].rearrange("(c p) f -> p c f", p=128), []))
                for pc in range(2):
                    ps = psf.next()
                    for tc in range(4):
                        P.tr(V(ps.t[:, 0, tc * 128:(tc + 1) * 128], ps.bufs), V(pstg_ap[:, tc, pc * 128:(pc + 1) * 128], xstg.bufs),
                             V(consts.t[:, 0, :], consts.bufs))
                    P.copy("act", V(pT.t[:, pc, :], pT.bufs), ps.c(0))
            for cb in range(2):
                wg = ws.get(l, "pleg", cb, 0, 8)
                wp = ws.get(l, "plep", cb, 0, 2)
                if ws.dry:
                    continue
                for m4 in range(4):
                    m = cb * 4 + m4
                    pg = psf.next()
                    proj(wg, m4, pg.c(0))
                    pe_ = psf.next()
                    for k in range(2):
                        P.mm(pe_.c(0), V(wp.t[:, k, m4 * 128:(m4 + 1) * 128], wp.bufs), V(pT.t[:, k, :], pT.bufs),
                             start=(k == 0), stop=(k == 1))
                    sg = f32r.next()
                    P.act(sg.c(0), pg.c(0), AF.Sigmoid)
                    P.tt("dve", sg.c(0), sg.c(0), pe_.c(0), ALU.mult)
                    P.tt("dve", resid.c(m), resid.c(m), sg.c(0), ALU.add)
                ws.release()

        def load_x(st):
            for tc in range(4):
                r0 = st * T + tc * 128
                P.dma("sp", V(xstg.t[:, 0, :], xstg.bufs), V(x_d[r0:r0 + 128, :], [xbuf_d]))
                for half in range(2):
                    ps = psf.next()
                    for c4 in range(4):
                        c = half * 4 + c4
                        P.tr(V(ps.t[:, 0, c4 * 128:(c4 + 1) * 128], ps.bufs), V(xstg.t[:, 0, c * 128:(c + 1) * 128], xstg.bufs),
                             V(consts.t[:, 0, :], consts.bufs))
                    P.copy("act", V(resid.t[:, half * 4:(half + 1) * 4, tc * 128:(tc + 1) * 128], resid.bufs[half * 4:(half + 1) * 4]),
                           V(ps.t[:, 0, :].rearrange("p (c t) -> p c t", c=4), ps.bufs))

        def store_out(st, normed):
            for tc in range(4):
                r0 = st * T + tc * 128
                for half in range(2):
                    ps = psf.next()
                    for c4 in range(4):
                        c = half * 4 + c4
                        P.tr(V(ps.t[:, 0, c4 * 128:(c4 + 1) * 128], ps.bufs), normed.c(c, slice(tc * 128, (tc + 1) * 128)),
                             V(consts.t[:, 0, :], consts.bufs))
                    P.copy("act", V(xstg.t[:, 0, half * 512:(half + 1) * 512], xstg.bufs), ps.c(0))
                P.dma("sp", V(out_d[r0:r0 + 128, :], [outbuf_d]), V(xstg.t[:, 0, :], xstg.bufs), sem_buf=xstg.bufs[0])

        def body():
            for st in range(NT):
                if not ws.dry:
                    load_x(st)
                for li, l in enumerate(layers):
                    def ca(k):
                        if st == 0 and not ws.dry:
                            conv_ahead(li * 4 + k + 2)
                    ca(0)
                    if "ffn1" in phases:
                        ffn(l, "ffn1", P_N1)
                    ca(1)
                    if "mixer" in phases:
                        mixer(l, st)
                    ca(2)
                    if "ffn2" in phases:
                        ffn(l, "ffn2", P_N2)
                    ca(3)
                    if "ple" in phases:
                        ple(l, st)
                if ws.dry:
                    continue
                if do_final:
                    rmsnorm(DEPTH * PL, out_f32=lambda c: resid.c(c))
                store_out(st, resid)

        body()
        ws.dry = False

        P.dma("sp", V(params.t[:, 0, :], params.bufs), V(par_d[:, :], []))
        P.dma("sp", V(consts.t[:], consts.bufs), V(cst_d[:, :, :], []))
        P.copy("dve", identb.c(0), V(consts.t[:, 0, :], consts.bufs))
        P.memset("dve", V(onesb.t[:, 0, :], onesb.bufs), 1.0 / 1024.0)
        P.memset("dve", V(onesb.t[:, 1, :], onesb.bufs), 1.0 / 512.0)
        P.copy("dve", cmb.c(0), V(consts.t[:, 3, :], consts.bufs))
        P.copy("dve", triUb.c(0), V(consts.t[:, 1, :], consts.bufs))
        for l in layers:
            P.memset("dve", states[l].all(), 0.0)
            P.memset("dve", tailm[l].all(), 0.0)
            P.memset("dve", tails[l].all(), 0.0)
            pb = l * PL
            an_l = V(aneg.t[:, 0, l * NH:(l + 1) * NH], aneg.bufs)
            P.act(an_l, par(pb + P_AL, NH), AF.Exp)
            P.ts("dve", an_l, an_l, -1.0, None, ALU.mult)

        body()

        E = P.eng["sp"]
        b = xstg.bufs[0]
        E.h.wait_ge(b.sem, b.cnt)
        print(f"[build] S={S} layers={layers} inst={P.ninst} waits={P.nwaits} wreqs={len(ws.reqs)}")
    return nc


def _pack_params(inp):
    par = np.zeros((128, NPAR), np.float32)

    def fm(v, nchunk):
        return np.ascontiguousarray(np.asarray(v, np.float32).reshape(nchunk, 128).T)

    for l in range(DEPTH):
        pb = l * PL
        par[:, pb + P_N1:pb + P_N1 + 8] = fm(inp["ffn1_norm"][l], 8)
        par[:, pb + P_NM:pb + P_NM + 8] = fm(inp["mix_norm"][l], 8)
        par[:, pb + P_N2:pb + P_N2 + 8] = fm(inp["ffn2_norm"][l], 8)
        par[:, pb + P_NP:pb + P_NP + 8] = fm(inp["ple_norm"][l], 8)
        scw = np.asarray(inp["sc_conv_w"][l], np.float32)
        par[:, pb + P_SCW:pb + P_SCW + 24] = scw.reshape(3, 8, 128).transpose(2, 1, 0).reshape(128, 24)
        mcw = np.asarray(inp["m_conv_w"][l], np.float32)
        par[:, pb + P_MCW:pb + P_MCW + 96] = mcw.reshape(4, 24, 128).transpose(2, 1, 0).reshape(128, 96)
        par[:, pb + P_MCB:pb + P_MCB + 24] = fm(inp["m_conv_b"][l], 24)
        par[:, pb + P_MN:pb + P_MN + 16] = fm(inp["m_norm"][l], 16)
        par[:, pb + P_DTB:pb + P_DTB + NH] = np.asarray(inp["m_dt_bias"][l], np.float32)[None, :]
        par[:, pb + P_AL:pb + P_AL + NH] = np.asarray(inp["m_A_log"][l], np.float32)[None, :]
        par[:, pb + P_DD:pb + P_DD + NH] = np.asarray(inp["m_D"][l], np.float32)[None, :]
    par[:, DEPTH * PL:DEPTH * PL + 8] = fm(inp["final_norm"], 8)
    return par


def _consts():
    c = np.zeros((128, 5, 128), np.float32)
    i = np.arange(128)
    c[:, 0, :] = np.eye(128, dtype=np.float32)
    c[:, 1, :] = (i[:, None] <= i[None, :]).astype(np.float32)
    c[:, 2, :] = (i[:, None] > i[None, :]).astype(np.float32)
    c[:, 3, :] = (i[None, :] >= i[:, None]).astype(np.float32)
    c[:, 4, :] = 1.0
    return c


_CACHE = {}


def run_layers(h_in, inp, layers, do_final, S, ncores, trace=False, phases=("ffn1", "mixer", "ffn2", "ple")):
    key = (S, tuple(layers), do_final, tuple(phases))
    if key not in _CACHE:
        _CACHE[key] = build_program(S, list(layers), do_final, phases=phases)
    nc = _CACHE[key]
    par = _pack_params(inp)
    cst = _consts()
    wts = {n: np.ascontiguousarray(np.asarray(inp[n], np.float32)[list(layers)]) for n in WSHAPES}
    p_all = np.asarray(inp["p"], np.float32)
    in_maps = []
    for b in range(ncores):
        m = {"x": np.ascontiguousarray(h_in[b]), "p": np.ascontiguousarray(p_all[:, b, :S, :]), "params": par, "consts": cst}
        m.update(wts)
        in_maps.append(m)
    res = run_bass_kernel_spmd(nc, in_maps, core_ids=list(range(ncores)), **({"trace": True} if trace else {}))
    out = np.stack([np.asarray(r["out"], np.float32) for r in res.results], axis=0)
    return out, res


def kernel(**inputs):
    x = np.asarray(inputs["x"], np.float32)
    out, _ = run_layers(x, inputs, [0, 1, 2, 3], True, 4096, 8)
    return out
```

```python
import numpy as np
import os
MK_STOP = float(os.environ.get('MK_STOP', '99'))
from contextlib import ExitStack
import concourse.bass as bass
import concourse.mybir as mybir
from concourse.bass_utils import run_bass_kernel_spmd

F32 = mybir.dt.float32
BF16 = mybir.dt.bfloat16
AF = mybir.ActivationFunctionType
ALU = mybir.AluOpType

D = 1024
DFF = 2816
NFC = DFF // 128
PLE = 256
NH = 32
T = 512
EPS = 1e-6
DEPTH = 4
PROJ = 10272
O_SCB, O_SCC, O_SCX, O_Z, O_XS, O_B, O_C, O_DT, O_GA, O_GM = 0, 1024, 2048, 3072, 5120, 7168, 7680, 8192, 8224, 9248
PL = 288
P_N1, P_NM, P_N2, P_NP, P_SCW, P_MCW, P_MCB, P_MN, P_DTB, P_AL, P_DD = 0, 8, 16, 24, 32, 56, 152, 176, 192, 224, 256
NPAR = DEPTH * PL + 8

SAME_ENGINE_SYNC = True
CONV_INFLIGHT = 2


class Buf:
    __slots__ = ("w", "r", "sem", "cnt", "name", "psum")

    def __init__(self, name=""):
        self.psum = False
        self.w = None
        self.r = []
        self.sem = None
        self.cnt = 0
        self.name = name


class V:
    __slots__ = ("ap", "bufs")

    def __init__(self, ap, bufs):
        self.ap = ap
        self.bufs = bufs if isinstance(bufs, (list, tuple)) else [bufs]


class Tn:
    def __init__(self, t, nb=1, name=""):
        self.t = t
        self.bufs = [Buf(f"{name}{i}") for i in range(nb)]

    def c(self, i, sl=None):
        ap = self.t[:, i] if sl is None else self.t[:, i, sl]
        return V(ap, [self.bufs[i]])

    def all(self):
        return V(self.t[:], self.bufs)

    def v(self, ap, idx=None):
        return V(ap, self.bufs if idx is None else [self.bufs[i] for i in idx])


class GSet:
    def __init__(self, tn_main, off_xs, off_z, tn_bc, off_bc):
        self.m, self.ox, self.oz, self.bc, self.ob = tn_main, off_xs, off_z, tn_bc, off_bc

    def c(self, i, sl=None):
        if i < 4:
            return self.m.c(self.ox + i, sl)
        if i < 6:
            return self.bc.c(self.ob + i - 4, sl)
        return self.m.c(self.oz + i - 6, sl)

    def zview(self, tsl):
        return V(self.m.t[:, self.oz:self.oz + 4, tsl], self.m.bufs[self.oz:self.oz + 4])


class Eng:
    def __init__(self, name, h, sem):
        self.name = name
        self.h = h
        self.sem = sem
        self.cnt = 0
        self.seen = {}


class Prog:
    def __init__(self, nc, es):
        self.nc = nc
        self.es = es
        self.sems = {}
        self.emitted = {}
        self.eng = {}
        for name, h in (("pe", nc.tensor), ("act", nc.scalar), ("dve", nc.vector), ("pool", nc.gpsimd), ("sp", nc.sync)):
            sem = self.newsem("e_" + name)
            self.eng[name] = Eng(name, h, sem)
        self.nwaits = 0
        self.ninst = 0

    def newsem(self, name):
        s = self.es.enter_context(self.nc.semaphore(name))
        self.emitted[id(s)] = 0
        self.sems[id(s)] = s
        return s

    def _wait(self, E, deps):
        best = {}
        for tok in deps:
            if tok is None:
                continue
            sem, val = tok
            k = id(sem)
            if best.get(k, (None, 0))[1] < val:
                best[k] = (sem, val)
        for k, (sem, val) in best.items():
            if sem is E.sem and (E.name in ("pe", "sp") or not SAME_ENGINE_SYNC):
                continue
            if E.seen.get(k, 0) >= val:
                continue
            assert self.emitted[k] >= val, f"wait on unemitted token {E.name} {val} > {self.emitted[k]}"
            E.h.wait_ge(sem, val)
            E.seen[k] = val
            self.nwaits += 1

    def emit(self, eng, fn, reads, writes, signal=True):
        E = self.eng[eng]
        deps = []
        for b in reads:
            deps.append(b.w)
            if b.psum:
                deps.extend(t for t in b.r if t[0] is not E.sem)
        for b in writes:
            deps.append(b.w)
            deps.extend(b.r)
        self._wait(E, deps)
        ins = fn(E.h)
        self.ninst += 1
        tok = (E.sem, E.cnt + 1)
        if signal:
            ins.then_inc(E.sem, 1)
            E.cnt += 1
            self.emitted[id(E.sem)] = E.cnt
        for b in reads:
            b.r.append(tok)
        for b in writes:
            b.w = tok
            b.r = []
        return ins

    def dma(self, q, out, in_, sem_buf=None, nodep=False):
        E = self.eng[q]
        deps = []
        for b in in_.bufs:
            deps.append(b.w)
        for b in out.bufs:
            deps.append(b.w)
            deps.extend(b.r)
        if not nodep:
            self._wait(E, deps)
        sb = sem_buf if sem_buf is not None else out.bufs[0]
        if sb.sem is None:
            sb.sem = self.newsem("d_" + sb.name)
        ins = E.h.dma_start(out=out.ap, in_=in_.ap)
        ins.then_inc(sb.sem, 16)
        sb.cnt += 16
        self.emitted[id(sb.sem)] = sb.cnt
        tok = (sb.sem, sb.cnt)
        for b in in_.bufs:
            b.r.append(tok)
        for b in out.bufs:
            b.w = tok
            b.r = []
        self.ninst += 1
        return tok

    def mm(self, out, lhsT, rhs, start=True, stop=True, signal=True):
        return self.emit("pe", lambda e: e.matmul(out.ap, lhsT=lhsT.ap, rhs=rhs.ap, start=start, stop=stop),
                         lhsT.bufs + rhs.bufs, out.bufs, signal)

    def tr(self, out, in_, ident, signal=True):
        return self.emit("pe", lambda e: e.transpose(out.ap, in_.ap, ident.ap), in_.bufs + ident.bufs, out.bufs, signal)

    def act(self, out, in_, func, bias=None, scale=1.0):
        rd = list(in_.bufs)
        kw = {}
        if bias is not None:
            if isinstance(bias, V):
                rd += bias.bufs
                kw["bias"] = bias.ap
            else:
                kw["bias"] = float(bias)
        if isinstance(scale, V):
            rd += scale.bufs
            kw["scale"] = scale.ap
        else:
            kw["scale"] = float(scale)
        return self.emit("act", lambda e: e.activation(out=out.ap, in_=in_.ap, func=func, **kw), rd, out.bufs)

    def tt(self, eng, out, in0, in1, op):
        return self.emit(eng, lambda e: e.tensor_tensor(out=out.ap, in0=in0.ap, in1=in1.ap, op=op),
                         in0.bufs + in1.bufs, out.bufs)

    def ts(self, eng, out, in0, s1, s2, op0, op1=None):
        rd = list(in0.bufs)
        a1 = s1
        a2 = s2
        if isinstance(s1, V):
            rd += s1.bufs
            a1 = s1.ap
        if isinstance(s2, V):
            rd += s2.bufs
            a2 = s2.ap
        if op1 is None:
            return self.emit(eng, lambda e: e.tensor_scalar(out=out.ap, in0=in0.ap, scalar1=a1, scalar2=None, op0=op0), rd, out.bufs)
        return self.emit(eng, lambda e: e.tensor_scalar(out=out.ap, in0=in0.ap, scalar1=a1, scalar2=a2, op0=op0, op1=op1), rd, out.bufs)

    def stt(self, out, in0, scalar, in1, op0, op1):
        rd = in0.bufs + in1.bufs
        a = scalar
        if isinstance(scalar, V):
            rd = rd + scalar.bufs
            a = scalar.ap
        return self.emit("dve", lambda e: e.scalar_tensor_tensor(out=out.ap, in0=in0.ap, scalar=a, in1=in1.ap, op0=op0, op1=op1),
                         rd, out.bufs)

    def copy(self, eng, out, in_):
        if eng == "act":
            return self.emit("act", lambda e: e.activation(out=out.ap, in_=in_.ap, func=AF.Copy), in_.bufs, out.bufs)
        return self.emit(eng, lambda e: e.tensor_copy(out=out.ap, in_=in_.ap), in_.bufs, out.bufs)

    def recip(self, out, in_):
        return self.emit("dve", lambda e: e.reciprocal(out=out.ap, in_=in_.ap), in_.bufs, out.bufs)

    def memset(self, eng, out, val):
        return self.emit(eng, lambda e: e.memset(out.ap, val), [], out.bufs)


class Ring:
    def __init__(self, items):
        self.items = items
        self.i = 0

    def next(self):
        x = self.items[self.i % len(self.items)]
        self.i += 1
        return x


def _blocks(c0, n, step=512):
    out = []
    c = 0
    while c < n:
        out.append((c0 + c, min(step, n - c)))
        c += step
    return out


WSPEC = {}
for _f in ("ffn1", "ffn2"):
    WSPEC[_f + "_wg"] = (_f + "_wg", D, _blocks(0, DFF))
    WSPEC[_f + "_wu"] = (_f + "_wu", D, _blocks(0, DFF))
    WSPEC[_f + "_wd"] = (_f + "_wd", DFF, _blocks(0, D))
WSPEC["dt"] = ("w_in", D, [(O_DT, 32)])
WSPEC["xs"] = ("w_in", D, _blocks(O_XS, 2048))
WSPEC["B"] = ("w_in", D, _blocks(O_B, 512, 128))
WSPEC["C"] = ("w_in", D, _blocks(O_C, 512, 128))
WSPEC["z"] = ("w_in", D, _blocks(O_Z, 2048))
WSPEC["scb"] = ("w_in", D, _blocks(O_SCB, 1024))
WSPEC["scc"] = ("w_in", D, _blocks(O_SCC, 1024))
WSPEC["scx"] = ("w_in", D, _blocks(O_SCX, 1024))
WSPEC["ga"] = ("w_in", D, _blocks(O_GA, 1024))
WSPEC["gm"] = ("w_in", D, _blocks(O_GM, 1024))
WSPEC["scwo"] = ("sc_w_out", D, _blocks(0, D))
WSPEC["mwo"] = ("m_w_out", 2048, _blocks(0, D))
WSPEC["wo"] = ("w_o", D, _blocks(0, D))
WSPEC["pleg"] = ("ple_w_gate", D, _blocks(0, D))
WSPEC["plep"] = ("ple_w_proj", PLE, _blocks(0, D))

WSHAPES = {"ffn1_wg": (D, DFF), "ffn1_wu": (D, DFF), "ffn1_wd": (DFF, D), "ffn2_wg": (D, DFF), "ffn2_wu": (D, DFF),
           "ffn2_wd": (DFF, D), "w_in": (D, PROJ), "sc_w_out": (D, D), "m_w_out": (2048, D), "w_o": (D, D),
           "ple_w_gate": (D, D), "ple_w_proj": (PLE, D)}
WORDER = ["ffn1_wg", "ffn1_wu", "ffn1_wd", "dt", "xs", "B", "C", "z", "scc", "scx", "scb", "scwo", "mwo", "ga", "gm", "wo",
          "ffn2_wg", "ffn2_wu", "ffn2_wd", "pleg", "plep"]


def build_program(S, layers, do_final, nslab=5, phases=("ffn1", "mixer", "ffn2", "ple")):
    NT = S // T
    nc = bass.Bass("TRN2", target_bir_lowering=False)
    es = ExitStack()
    with es:
        P = Prog(nc, es)

        def dram(name, shape, dt, kind):
            return nc.dram_tensor(name, list(shape), dt, kind=kind).ap()

        def sb(name, shape, dt):
            return es.enter_context(nc.sbuf_tensor(name, list(shape), dt))

        def pp(name, shape, dt):
            return es.enter_context(nc.psum_tensor(name, list(shape), dt))

        x_d = dram("x", (S, D), F32, "ExternalInput")
        p_d = dram("p", (DEPTH, S, PLE), F32, "ExternalInput")
        par_d = dram("params", (128, NPAR), F32, "ExternalInput")
        cst_d = dram("consts", (128, 5, 128), F32, "ExternalInput")
        out_d = dram("out", (S, D), F32, "ExternalOutput")
        w_d = {n: dram(n, (len(layers),) + shp, F32, "ExternalInput") for n, shp in WSHAPES.items()}
        xbuf_d = Buf("x_d")
        outbuf_d = Buf("out_d")

        scr = {}
        for l in layers:
            for name in WORDER:
                src, K, blks = WSPEC[name]
                kc = K // 128
                t = dram(f"scr_{l}_{name}", (len(blks), 128, kc, 512), BF16, "Internal")
                scr[(l, name)] = (t, Buf(f"scr{l}{name}"), kc, blks)

        resid = Tn(sb("resid", (128, 8, T), F32), 8, "res")
        xn = Tn(sb("xn", (128, 8, T), BF16), 8, "xn")
        actp = Tn(sb("actp", (128, 24, T), BF16), 24, "act")
        grpA = Tn(sb("grpA", (128, 10, T), BF16), 10, "grpA")
        bc2 = Tn(sb("bc2", (128, 2, T), BF16), 2, "bc2")
        pT = bc2
        gy = Tn(sb("gy", (128, 4, T), F32), 4, "gy")
        states = {l: Tn(sb(f"st{l}", (128, 4, T), F32), 4, f"st{l}") for l in layers}
        sbf = Tn(sb("sbf", (128, 4, T), BF16), 4, "sbf")
        tailm = {l: Tn(sb(f"tm{l}", (128, 24, 3), F32), 24, f"tm{l}") for l in layers}
        tails = {l: Tn(sb(f"ts{l}", (128, 8, 2), F32), 8, f"ts{l}") for l in layers}
        cvr = Ring([Tn(sb(f"cv{i}", (128, 1, T + 3), F32), 1, f"cv{i}") for i in range(2)])
        f32r = Ring([Tn(sb(f"f32r{i}", (128, 1, T), F32), 1, f"f32r{i}") for i in range(4)])
        rstd = Tn(sb("rstd", (128, 1, T), F32), 1, "rstd")
        slabs = [Tn(sb(f"slab{i}", (128, 8, 512), BF16), 1, f"slab{i}") for i in range(nslab)]
        params = Tn(sb("params_sb", (128, 1, NPAR), F32), 1, "params")
        consts = Tn(sb("consts_sb", (128, 5, 128), F32), 1, "consts")
        identb = Tn(sb("identb", (128, 1, 128), BF16), 1, "identb")
        onesb = Tn(sb("onesb", (128, 2, 128), BF16), 1, "onesb")
        cmb = Tn(sb("cmb", (128, 1, 128), F32), 1, "cmb")
        aneg = Tn(sb("aneg", (128, 1, DEPTH * NH), F32), 1, "aneg")
        xstg = Tn(sb("xstg", (128, 1, D), F32), 1, "xstg")
        dtt = {n: Tn(sb("dt_" + n, (128, 4, NH), F32), 1, "dt_" + n) for n in
               ("t", "ab", "ex", "dt", "a", "acum", "dif", "dte", "cd")}
        dtt["dtdte"] = dtt["t"]
        dtt["E"] = dtt["ab"]
        Xb = Tn(sb("Xb", (128, 4, T), BF16), 4, "Xb")
        Xdb = Tn(sb("Xdb", (128, 3, T), BF16), 3, "Xdb")
        xsD = Tn(sb("xsD", (128, 4, T), BF16), 4, "xsD")
        BTb = Tn(sb("BTb", (128, 3, 128), BF16), 3, "BTb")
        CBm = Tn(sb("CBm", (128, 3, 128), BF16), 3, "CBm")
        Lfr = Ring([Tn(sb(f"Lf{i}", (128, 8, 128), BF16), 1, f"Lf{i}") for i in range(2)])
        triUb = Tn(sb("triUb", (128, 1, 128), BF16), 1, "triUb")
        dec = Tn(sb("dec", (128, 4, 1024), BF16), 4, "dec")
        Mh = dec
        ybf = Tn(sb("ybf", (128, 2, T), BF16), 2, "ybf")
        accr = Ring([Tn(sb(f"accr{i}", (128, 1, T), F32), 1, f"accr{i}") for i in range(2)])
        print("[build] sbuf bytes remaining", nc.sbuf_bytes_remaining)

        ps6 = [Tn(pp(f"psf{i}", (128, 1, 512), F32), 1, f"psf{i}") for i in range(6)]
        psf = Ring(ps6)
        pss = Ring(ps6[:4])
        psp = Ring(ps6[4:])
        psb = Ring([Tn(pp(f"psb{i}", (128, 1, 1024), BF16), 1, f"psb{i}") for i in range(2)])
        for tn_ in psf.items + psb.items:
            tn_.bufs[0].psum = True

        conv_hist = []
        PH_NAMES = {"ffn1": ["ffn1_wg", "ffn1_wu", "ffn1_wd"],
                    "mixer": ["dt", "xs", "B", "C", "z", "scc", "scx", "scb", "scwo", "mwo", "ga", "gm", "wo"],
                    "ffn2": ["ffn2_wg", "ffn2_wu", "ffn2_wd"], "ple": ["pleg", "plep"]}
        conv_seq = [(l, ph) for l in layers for ph in ("ffn1", "mixer", "ffn2", "ple")]
        conv_state = {"pos": 0}

        def conv_ahead(n):
            while conv_state["pos"] < min(len(conv_seq), n):
                l, ph = conv_seq[conv_state["pos"]]
                conv_state["pos"] += 1
                for name in PH_NAMES[ph]:
                    if name not in WORDER:
                        continue
                    src, K, blks = WSPEC[name]
                    t, buf, kc, _ = scr[(l, name)]
                    if len(conv_hist) >= CONV_INFLIGHT:
                        pb_ = conv_hist[-CONV_INFLIGHT]
                        P.eng["pool"].h.wait_ge(pb_.sem, pb_.cnt)
                    conv_hist.append(buf)
                    for bi, (c0, ncol) in enumerate(blks):
                        for k0 in range(0, kc, 8):
                            k1 = min(kc, k0 + 8)
                            src_ap = w_d[src][layers.index(l), k0 * 128:k1 * 128, c0:c0 + ncol].rearrange("(k p) c -> p k c", p=128)
                            P.dma("pool", V(t[bi, :, k0:k1, 0:ncol], [buf]), V(src_ap, []), nodep=True)

        class WS:
            def __init__(self):
                self.reqs = []
                self.dry = True
                self.i = 0
                self.loaded = 0
                self.released = 0

            def get(self, l, name, bi, k0, k1):
                if self.dry:
                    self.reqs.append((l, name, bi, k0, k1))
                    return None
                i = self.i
                assert self.reqs[i] == (l, name, bi, k0, k1)
                assert i - nslab < self.released, "too many slabs held"
                self._pump()
                assert self.loaded > i
                self.i += 1
                return slabs[i % nslab]

            def _pump(self):
                while self.loaded < min(len(self.reqs), self.released + nslab):
                    self._load(self.loaded)
                    self.loaded += 1

            def release(self):
                if self.dry:
                    return
                self.released = self.i
                self._pump()

            def _load(self, j):
                l, name, bi, k0, k1 = self.reqs[j]
                t, buf, kc, blks = scr[(l, name)]
                ncol = blks[bi][1]
                sl = slabs[j % nslab]
                P.dma("sp", V(sl.t[:, 0:k1 - k0, 0:ncol], sl.bufs), V(t[bi, :, k0:k1, 0:ncol], [buf]))

        ws = WS()

        def par(col, n=1):
            return V(params.t[:, 0, col:col + n], params.bufs)

        stats = {"ps": None, "n": 0}

        def stats_sq(chunks):
            for c in chunks:
                P.act(xn.c(c), resid.c(c), AF.Square)

        def stats_mm(chunks):
            for c in chunks:
                if stats["n"] == 0:
                    stats["ps"] = psf.next()
                P.mm(stats["ps"].c(0), V(onesb.t[:, 0, :], onesb.bufs), xn.c(c), start=(stats["n"] == 0), stop=(stats["n"] == 7))
                stats["n"] += 1

        def rmsnorm(l_col, out_f32=None):
            if ws.dry:
                return
            if stats["n"] == 0:
                stats_sq(range(8))
                stats_mm(range(8))
            assert stats["n"] == 8
            ps = stats["ps"]
            stats["n"] = 0
            stats["ps"] = None
            rstd_from(ps)
            for c in range(8):
                o = xn.c(c) if out_f32 is None else out_f32(c)
                P.stt(o, resid.c(c), par(l_col + c), rstd.c(0), ALU.mult, ALU.mult)

        def ffn(l, pre, ncol_norm):
            rmsnorm(l * PL + ncol_norm)
            blks = WSPEC[pre + "_wg"][2]
            for bi, (c0, ncol) in enumerate(blks):
                wg = ws.get(l, pre + "_wg", bi, 0, 8)
                wu = ws.get(l, pre + "_wu", bi, 0, 8)
                if ws.dry:
                    continue
                for fc in range(ncol // 128):
                    j = bi * 4 + fc
                    pg = psf.next()
                    for k in range(8):
                        P.mm(pg.c(0), V(wg.t[:, k, fc * 128:(fc + 1) * 128], wg.bufs), xn.c(k), start=(k == 0), stop=(k == 7))
                    pu = psf.next()
                    for k in range(8):
                        P.mm(pu.c(0), V(wu.t[:, k, fc * 128:(fc + 1) * 128], wu.bufs), xn.c(k), start=(k == 0), stop=(k == 7))
                    sg = f32r.next()
                    P.act(sg.c(0), pg.c(0), AF.Silu)
                    P.tt("dve", actp.c(j), sg.c(0), pu.c(0), ALU.mult)
                ws.release()
            for cb in range(2):
                pss = [psf.next() for _ in range(4)] if not ws.dry else None
                for k0 in range(0, NFC, 8):
                    k1 = min(NFC, k0 + 8)
                    wd = ws.get(l, pre + "_wd", cb, k0, k1)
                    if ws.dry:
                        continue
                    for kk in range(k1 - k0):
                        j = k0 + kk
                        for m4 in range(4):
                            P.mm(pss[m4].c(0), V(wd.t[:, kk, m4 * 128:(m4 + 1) * 128], wd.bufs), actp.c(j),
                                 start=(j == 0), stop=(j == NFC - 1))
                    ws.release()
                if ws.dry:
                    continue
                if cb == 1:
                    stats_mm(range(0, 4))
                for m4 in range(4):
                    m = cb * 4 + m4
                    P.stt(resid.c(m), pss[m4].c(0), 0.5, resid.c(m), ALU.mult, ALU.add)
                stats_sq(range(cb * 4, cb * 4 + 4))
                if cb == 1:
                    stats_mm(range(4, 8))

        def proj(slab, fc, out_ps):
            for k in range(8):
                P.mm(out_ps, V(slab.t[:, k, fc * 128:(fc + 1) * 128], slab.bufs), xn.c(k), start=(k == 0), stop=(k == 7))

        def bc_h(tn, tc, g):
            return V(tn.t[:, tc, g * 8:(g + 1) * 8].unsqueeze(2).to_broadcast([128, 8, 64]), tn.bufs)

        def rstd_from(ps):
            s_ = f32r.next()
            P.act(s_.c(0), ps.c(0), AF.Ln, bias=EPS)
            P.act(rstd.c(0), s_.c(0), AF.Exp, scale=-0.5)

        def proj_get(l, g):
            return (ws.get(l, "xs", g, 0, 8), ws.get(l, "B", g, 0, 8), ws.get(l, "C", g, 0, 8), ws.get(l, "z", g, 0, 8))

        def proj_units(l, g, G, sl4):
            pb = l * PL
            wxs, wB, wC, wz = sl4
            pend = []
            for i in range(6):
                if i < 4:
                    slab, fc, ch = wxs, i, g * 4 + i
                elif i == 4:
                    slab, fc, ch = wB, 0, 16 + g
                else:
                    slab, fc, ch = wC, 0, 20 + g
                ps = psp.next()
                proj(slab, fc, ps.c(0))
                cv = cvr.next()
                tl = tailm[l]
                wc = pb + P_MCW + ch * 4
                acc = accr.next()
                P.act(acc.c(0), ps.c(0), AF.Identity, bias=par(pb + P_MCB + ch), scale=par(wc + 3))
                P.copy("pool", V(cv.t[:, 0, 0:3], cv.bufs), V(tl.t[:, ch, :], [tl.bufs[ch]]))
                P.copy("act", V(cv.t[:, 0, 3:3 + T], cv.bufs), ps.c(0))
                P.copy("pool", V(tl.t[:, ch, :], [tl.bufs[ch]]), V(cv.t[:, 0, T:T + 3], cv.bufs))
                p2 = f32r.next()
                P.act(p2.c(0), V(cv.t[:, 0, 2:2 + T], cv.bufs), AF.Identity, scale=par(wc + 2))
                for (o_, a_) in pend:
                    P.act(o_, a_, AF.Silu)
                pend = [(G.c(i), acc.c(0))]
                P.tt("pool", acc.c(0), acc.c(0), p2.c(0), ALU.add)
                for k in range(0, 2):
                    P.stt(acc.c(0), V(cv.t[:, 0, k:k + T], cv.bufs), par(wc + k), acc.c(0), ALU.mult, ALU.add)
                yield
            for fc in range(4):
                ps = psp.next()
                proj(wz, fc, ps.c(0))
                for (o_, a_) in pend:
                    P.act(o_, a_, AF.Silu)
                pend = []
                P.act(G.c(6 + fc), ps.c(0), AF.Silu)
                if fc == 3:
                    ws.release()
                yield

        def silu_batch(G):
            for i in range(10):
                P.act(G.c(i), G.c(i), AF.Silu)

        def mixer(l, st):
            pb = l * PL
            rmsnorm(pb + P_NM)
            d = dtt
            v4 = lambda n: V(d[n].t[:], d[n].bufs)
            gsets = [GSet(grpA, 0, 6, grpA, 4), GSet(actp, 16, 20, bc2, 0)]
            wdt = ws.get(l, "dt", 0, 0, 8)
            if not ws.dry:
                ps = psf.next()
                for tc in range(4):
                    for k in range(8):
                        P.mm(V(ps.t[:, 0, tc * 32:(tc + 1) * 32], ps.bufs), V(xn.t[:, k, tc * 128:(tc + 1) * 128], [xn.bufs[k]]),
                             V(wdt.t[:, k, 0:32], wdt.bufs), start=(k == 0), stop=(k == 7))
                ps4 = V(ps.t[:, 0, 0:128].rearrange("p (c h) -> p c h", c=4), ps.bufs)
                bias_b = V(params.t[:, 0, pb + P_DTB:pb + P_DTB + NH].unsqueeze(1).to_broadcast([128, 4, NH]), params.bufs)
                P.tt("dve", v4("t"), ps4, bias_b, ALU.add)
                ws.release()
                P.act(v4("ab"), v4("t"), AF.Abs)
                P.act(v4("ex"), v4("ab"), AF.Exp, scale=-1.0)
                P.act(v4("ab"), v4("ex"), AF.Ln, bias=1.0)
                P.stt(v4("dt"), v4("t"), 0.0, v4("ab"), ALU.max, ALU.add)
                an_b = V(aneg.t[:, 0, l * NH:(l + 1) * NH].unsqueeze(1).to_broadcast([128, 4, NH]), aneg.bufs)
                P.tt("dve", v4("a"), v4("dt"), an_b, ALU.mult)
            sl4 = proj_get(l, 0)
            if not ws.dry:
                for _ in proj_units(l, 0, gsets[0], sl4):
                    pass
                pa = psf.next()
                pt = psf.next()
                for tc in range(4):
                    P.mm(V(pa.t[:, 0, tc * 32:(tc + 1) * 32], pa.bufs), V(consts.t[:, 1, :], consts.bufs),
                         V(d["a"].t[:, tc, :], d["a"].bufs))
                    P.mm(V(pt.t[:, 0, tc * 32:(tc + 1) * 32], pt.bufs), V(consts.t[:, 4, :], consts.bufs),
                         V(d["a"].t[:, tc, :], d["a"].bufs))
                pa4 = V(pa.t[:, 0, 0:128].rearrange("p (c h) -> p c h", c=4), pa.bufs)
                pt4 = V(pt.t[:, 0, 0:128].rearrange("p (c h) -> p c h", c=4), pt.bufs)
                P.copy("dve", v4("acum"), pa4)
                P.act(v4("E"), v4("acum"), AF.Exp)
                P.tt("dve", v4("dif"), pt4, v4("acum"), ALU.subtract)
                P.act(v4("dte"), v4("dif"), AF.Exp)
                P.tt("dve", v4("ex"), v4("dif"), v4("acum"), ALU.add)
                P.act(v4("cd"), v4("ex"), AF.Exp)
                P.tt("dve", v4("dtdte"), v4("dt"), v4("dte"), ALU.mult)

            pend_norm = {"g": None}

            def group_norm():
                gg = pend_norm["g"]
                if gg is None or ws.dry:
                    return
                pend_norm["g"] = None
                ps = pss.next()
                for fc in range(4):
                    P.act(actp.c(gg * 4 + fc), gy.c(fc), AF.Square)
                    P.mm(ps.c(0), V(onesb.t[:, 1, :], onesb.bufs), actp.c(gg * 4 + fc), start=(fc == 0), stop=(fc == 3))
                rstd_from(ps)
                for fc in range(4):
                    P.stt(actp.c(gg * 4 + fc), gy.c(fc), par(pb + P_MN + gg * 4 + fc), rstd.c(0), ALU.mult, ALU.mult)

            for g in range(4):
                G = gsets[g % 2]
                gen = None
                if g < 3:
                    sl4 = proj_get(l, g + 1)
                    if not ws.dry:
                        gen = proj_units(l, g + 1, gsets[(g + 1) % 2], sl4)
                if ws.dry:
                    continue

                def pump(n=1):
                    if gen is not None:
                        for _ in range(n):
                            next(gen, None)

                S_ = states[l]
                v3 = lambda tn, i: V(tn.t[:, i, :].rearrange("p (h d) -> p h d", h=8), [tn.bufs[i]])
                P.copy("act", sbf.c(0), S_.c(g))
                Lfs = {}
                pxss = {}

                def t_evac(tc):
                    tsl = slice(tc * 128, (tc + 1) * 128)
                    pxs = psb.next()
                    for fc in range(4):
                        P.tr(V(pxs.t[:, 0, fc * 128:(fc + 1) * 128], pxs.bufs), G.c(fc, tsl), identb.c(0))
                    P.tr(V(pxs.t[:, 0, 512:640], pxs.bufs), G.c(4, tsl), identb.c(0))
                    pcb = pss.next()
                    P.mm(V(pcb.t[:, 0, 0:128], pcb.bufs), G.c(4, tsl), G.c(5, tsl))
                    pxs3 = V(pxs.t[:, 0, 0:T].rearrange("p (h d) -> p h d", h=8), pxs.bufs)
                    P.tt("dve", v3(Xb, tc), pxs3, bc_h(d["dt"], tc, g), ALU.mult)
                    P.tt("dve", v3(Xdb, tc % 3), pxs3, bc_h(d["dtdte"], tc, g), ALU.mult)
                    dd_b = V(params.t[:, 0, pb + P_DD + g * 8:pb + P_DD + (g + 1) * 8].unsqueeze(2).to_broadcast([128, 8, 64]), params.bufs)
                    P.tt("dve", v3(xsD, tc), pxs3, dd_b, ALU.mult)
                    P.copy("dve", BTb.c(tc % 3), V(pxs.t[:, 0, 512:640], pxs.bufs))
                    P.tt("dve", CBm.c(tc % 3), V(pcb.t[:, 0, 0:128], pcb.bufs), cmb.c(0), ALU.mult)

                def mk_Lf(tc):
                    Lf = Lfr.next()
                    Lfs[tc] = Lf
                    ms_b = V(consts.t[:, 2, :].unsqueeze(1).to_broadcast([128, 8, 128]), consts.bufs)
                    a_b = V(d["a"].t[:, tc, g * 8:(g + 1) * 8].unsqueeze(2).to_broadcast([128, 8, 128]), d["a"].bufs)
                    P.tt("pool", V(Lf.t[:], Lf.bufs), ms_b, a_b, ALU.mult)

                def decay(tc):
                    Lf = Lfs[tc]
                    for hb in range(2):
                        pd = pss.next()
                        for hh in range(4):
                            h = hb * 4 + hh
                            P.mm(V(pd.t[:, 0, hh * 128:(hh + 1) * 128], pd.bufs), V(Lf.t[:, h, :], Lf.bufs),
                                 triUb.c(0))
                        P.act(V(dec.t[:, tc, hb * 512:(hb + 1) * 512].rearrange("p (h s) -> p h s", h=4), [dec.bufs[tc]]),
                              V(pd.t[:, 0, :].rearrange("p (h s) -> p h s", h=4), pd.bufs), AF.Exp)
                    cb_b = V(CBm.t[:, tc % 3, :].unsqueeze(1).to_broadcast([128, 8, 128]), [CBm.bufs[tc % 3]])
                    P.tt("dve", V(Mh.t[:, tc, :].rearrange("p (h s) -> p h s", h=8), [Mh.bufs[tc]]),
                         V(dec.t[:, tc, :].rearrange("p (h s) -> p h s", h=8), [dec.bufs[tc]]), cb_b, ALU.mult)

                def state(tc):
                    pst = pss.next()
                    P.mm(pst.c(0), BTb.c(tc % 3), Xdb.c(tc % 3))
                    s3 = V(S_.t[:, g, :].rearrange("p (h d) -> p h d", h=8), [S_.bufs[g]])
                    P.tt("pool", s3, s3, bc_h(d["cd"], tc, g), ALU.mult)
                    P.tt("dve", S_.c(g), S_.c(g), pst.c(0), ALU.add)
                    if tc < 3:
                        P.copy("act", sbf.c(tc + 1), S_.c(g))

                def ycomb(tc):
                    tsl = slice(tc * 128, (tc + 1) * 128)
                    py = pss.next()
                    for h in range(8):
                        P.mm(V(py.t[:, 0, h * 64:(h + 1) * 64], py.bufs), V(Mh.t[:, tc, h * 128:(h + 1) * 128], [Mh.bufs[tc]]),
                             V(Xb.t[:, tc, h * 64:(h + 1) * 64], [Xb.bufs[tc]]))
                    pyo = pss.next()
                    P.mm(pyo.c(0), G.c(5, tsl), sbf.c(tc))
                    t1 = f32r.next()
                    pyo3 = V(pyo.t[:, 0, :].rearrange("p (h d) -> p h d", h=8), pyo.bufs)
                    P.tt("dve", V(t1.t[:, 0, :].rearrange("p (h d) -> p h d", h=8), t1.bufs), pyo3, bc_h(d["E"], tc, g), ALU.mult)
                    P.tt("dve", t1.c(0), t1.c(0), py.c(0), ALU.add)
                    P.tt("pool", ybf.c(tc % 2), t1.c(0), xsD.c(tc), ALU.add)

                def ygate(tc):
                    tsl = slice(tc * 128, (tc + 1) * 128)
                    pyt = psb.next()
                    for fc in range(4):
                        P.tr(V(pyt.t[:, 0, fc * 128:(fc + 1) * 128], pyt.bufs), V(ybf.t[:, tc % 2, fc * 128:(fc + 1) * 128], [ybf.bufs[tc % 2]]),
                             identb.c(0))
                    P.tt("dve", V(gy.t[:, :, tsl], gy.bufs), V(pyt.t[:, 0, 0:T].rearrange("p (f t) -> p f t", f=4), pyt.bufs),
                         G.zview(tsl), ALU.mult)

                t_evac(0)
                t_evac(1)
                mk_Lf(0)
                for tc in range(4):
                    if tc + 2 < 4:
                        t_evac(tc + 2)
                    if tc + 1 < 4:
                        mk_Lf(tc + 1)
                    decay(tc)
                    state(tc)
                    if tc == 1:
                        group_norm()
                for tc in range(4):
                    ycomb(tc)
                    pump(2)
                    if tc >= 1:
                        ygate(tc - 1)
                pump(10)
                ygate(3)
                pend_norm["g"] = g

            for cb in range(2):
                wc_ = ws.get(l, "scc", cb, 0, 8)
                wx_ = ws.get(l, "scx", cb, 0, 8)
                wb_ = ws.get(l, "scb", cb, 0, 8)
                if ws.dry:
                    continue
                for fc in range(4):
                    ch = cb * 4 + fc
                    pc = psf.next()
                    proj(wc_, fc, pc.c(0))
                    px = psf.next()
                    proj(wx_, fc, px.c(0))
                    pbb = psf.next()
                    proj(wb_, fc, pbb.c(0))
                    csb = f32r.next()
                    P.copy("act", csb.c(0), pc.c(0))
                    cv = cvr.next()
                    tl = tails[l]
                    P.copy("pool", V(cv.t[:, 0, 0:2], cv.bufs), V(tl.t[:, ch, :], [tl.bufs[ch]]))
                    P.tt("dve", V(cv.t[:, 0, 2:2 + T], cv.bufs), csb.c(0), px.c(0), ALU.mult)
                    P.copy("pool", V(tl.t[:, ch, :], [tl.bufs[ch]]), V(cv.t[:, 0, T:T + 2], cv.bufs))
                    acc = f32r.next()
                    wc = pb + P_SCW + ch * 3
                    P.act(acc.c(0), V(cv.t[:, 0, 0:T], cv.bufs), AF.Identity, scale=par(wc))
                    for k in range(1, 3):
                        P.stt(acc.c(0), V(cv.t[:, 0, k:k + T], cv.bufs), par(wc + k), acc.c(0), ALU.mult, ALU.add)
                    P.tt("dve", actp.c(16 + ch), acc.c(0), pbb.c(0), ALU.mult)
                ws.release()
                group_norm()

            for cb in range(2):
                wsc = ws.get(l, "scwo", cb, 0, 8)
                wga = ws.get(l, "ga", cb, 0, 8)
                if not ws.dry:
                    for m4 in range(4):
                        m = cb * 4 + m4
                        msl = slice(m4 * 128, (m4 + 1) * 128)
                        pga = psf.next()
                        proj(wga, m4, pga.c(0))
                        pya = psf.next()
                        for k in range(8):
                            P.mm(pya.c(0), V(wsc.t[:, k, msl], wsc.bufs), actp.c(16 + k), start=(k == 0), stop=(k == 7))
                        sa = f32r.next()
                        P.act(sa.c(0), pga.c(0), AF.Sigmoid)
                        P.tt("dve", grpA.c(m), sa.c(0), pya.c(0), ALU.mult)
                    ws.release()
                wm0 = ws.get(l, "mwo", cb, 0, 8)
                wm1 = ws.get(l, "mwo", cb, 8, 16)
                wgm = ws.get(l, "gm", cb, 0, 8)
                if ws.dry:
                    continue
                for m4 in range(4):
                    m = cb * 4 + m4
                    msl = slice(m4 * 128, (m4 + 1) * 128)
                    pgm = psf.next()
                    proj(wgm, m4, pgm.c(0))
                    pym = psf.next()
                    for k in range(16):
                        wm = wm0 if k < 8 else wm1
                        P.mm(pym.c(0), V(wm.t[:, k % 8, msl], wm.bufs), actp.c(k), start=(k == 0), stop=(k == 15))
                    sm = f32r.next()
                    P.act(sm.c(0), pgm.c(0), AF.Sigmoid)
                    P.tt("dve", sm.c(0), sm.c(0), pym.c(0), ALU.mult)
                    P.tt("pool", grpA.c(m), grpA.c(m), sm.c(0), ALU.add)
                ws.release()
            for cb in range(2):
                wo = ws.get(l, "wo", cb, 0, 8)
                if ws.dry:
                    continue
                for m4 in range(4):
                    m = cb * 4 + m4
                    po = psf.next()
                    for k in range(8):
                        P.mm(po.c(0), V(wo.t[:, k, m4 * 128:(m4 + 1) * 128], wo.bufs), grpA.c(k), start=(k == 0), stop=(k == 7))
                    P.tt("dve", resid.c(m), po.c(0), resid.c(m), ALU.add)
                ws.release()
                if cb == 1:
                    stats_mm(range(0, 4))
                stats_sq(range(cb * 4, cb * 4 + 4))
                if cb == 1:
                    stats_mm(range(4, 8))

        def ple(l, st):
            pb = l * PL
            rmsnorm(pb + P_NP)
            if not ws.dry:
                pstg_ap = xstg.t[:, 0, :].rearrange("p (c f) -> p c f", c=4)
                P.dma("sp", V(pstg_ap, xstg.bufs),
                      V(p_d[l, st * T:(st + 1) * T, :].rearrange("(c p) f -> p c f", p=128), []))
                for pc in range(2):
                    ps = psf.next()
                    for tc in range(4):
                        P.tr(V(ps.t[:, 0, tc * 128:(tc + 1) * 128], ps.bufs), V(pstg_ap[:, tc, pc * 128:(pc + 1) * 128], xstg.bufs),
                             V(consts.t[:, 0, :], consts.bufs))
                    P.copy("act", V(pT.t[:, pc, :], pT.bufs), ps.c(0))
            for cb in range(2):
                wg = ws.get(l, "pleg", cb, 0, 8)
                wp = ws.get(l, "plep", cb, 0, 2)
                if ws.dry:
                    continue
                for m4 in range(4):
                    m = cb * 4 + m4
                    pg = psf.next()
                    proj(wg, m4, pg.c(0))
                    pe_ = psf.next()
                    for k in range(2):
                        P.mm(pe_.c(0), V(wp.t[:, k, m4 * 128:(m4 + 1) * 128], wp.bufs), V(pT.t[:, k, :], pT.bufs),
                             start=(k == 0), stop=(k == 1))
                    sg = f32r.next()
                    P.act(sg.c(0), pg.c(0), AF.Sigmoid)
                    P.tt("dve", sg.c(0), sg.c(0), pe_.c(0), ALU.mult)
                    P.tt("dve", resid.c(m), resid.c(m), sg.c(0), ALU.add)
                ws.release()

        def load_x(st):
            for tc in range(4):
                r0 = st * T + tc * 128
                P.dma("sp", V(xstg.t[:, 0, :], xstg.bufs), V(x_d[r0:r0 + 128, :], [xbuf_d]))
                for half in range(2):
                    ps = psf.next()
                    for c4 in range(4):
                        c = half * 4 + c4
                        P.tr(V(ps.t[:, 0, c4 * 128:(c4 + 1) * 128], ps.bufs), V(xstg.t[:, 0, c * 128:(c + 1) * 128], xstg.bufs),
                             V(consts.t[:, 0, :], consts.bufs))
                    P.copy("act", V(resid.t[:, half * 4:(half + 1) * 4, tc * 128:(tc + 1) * 128], resid.bufs[half * 4:(half + 1) * 4]),
                           V(ps.t[:, 0, :].rearrange("p (c t) -> p c t", c=4), ps.bufs))

        def store_out(st, normed):
            for tc in range(4):
                r0 = st * T + tc * 128
                for half in range(2):
                    ps = psf.next()
                    for c4 in range(4):
                        c = half * 4 + c4
                        P.tr(V(ps.t[:, 0, c4 * 128:(c4 + 1) * 128], ps.bufs), normed.c(c, slice(tc * 128, (tc + 1) * 128)),
                             V(consts.t[:, 0, :], consts.bufs))
                    P.copy("act", V(xstg.t[:, 0, half * 512:(half + 1) * 512], xstg.bufs), ps.c(0))
                P.dma("sp", V(out_d[r0:r0 + 128, :], [outbuf_d]), V(xstg.t[:, 0, :], xstg.bufs), sem_buf=xstg.bufs[0])

        def body():
            for st in range(NT):
                if not ws.dry:
                    load_x(st)
                for li, l in enumerate(layers):
                    def ca(k):
                        if st == 0 and not ws.dry:
                            conv_ahead(li * 4 + k + 2)
                    ca(0)
                    if "ffn1" in phases:
                        ffn(l, "ffn1", P_N1)
                    ca(1)
                    if "mixer" in phases:
                        mixer(l, st)
                    ca(2)
                    if "ffn2" in phases:
                        ffn(l, "ffn2", P_N2)
                    ca(3)
                    if "ple" in phases:
                        ple(l, st)
                if ws.dry:
                    continue
                if do_final:
                    rmsnorm(DEPTH * PL, out_f32=lambda c: resid.c(c))
                store_out(st, resid)

        body()
        ws.dry = False

        P.dma("sp", V(params.t[:, 0, :], params.bufs), V(par_d[:, :], []))
        P.dma("sp", V(consts.t[:], consts.bufs), V(cst_d[:, :, :], []))
        P.copy("dve", identb.c(0), V(consts.t[:, 0, :], consts.bufs))
        P.memset("dve", V(onesb.t[:, 0, :], onesb.bufs), 1.0 / 1024.0)
        P.memset("dve", V(onesb.t[:, 1, :], onesb.bufs), 1.0 / 512.0)
        P.copy("dve", cmb.c(0), V(consts.t[:, 3, :], consts.bufs))
        P.copy("dve", triUb.c(0), V(consts.t[:, 1, :], consts.bufs))
        for l in layers:
            P.memset("dve", states[l].all(), 0.0)
            P.memset("dve", tailm[l].all(), 0.0)
            P.memset("dve", tails[l].all(), 0.0)
            pb = l * PL
            an_l = V(aneg.t[:, 0, l * NH:(l + 1) * NH], aneg.bufs)
            P.act(an_l, par(pb + P_AL, NH), AF.Exp)
            P.ts("dve", an_l, an_l, -1.0, None, ALU.mult)

        body()

        E = P.eng["sp"]
        b = xstg.bufs[0]
        E.h.wait_ge(b.sem, b.cnt)
        print(f"[build] S={S} layers={layers} inst={P.ninst} waits={P.nwaits} wreqs={len(ws.reqs)}")
    return nc


def _pack_params(inp):
    par = np.zeros((128, NPAR), np.float32)

    def fm(v, nchunk):
        return np.ascontiguousarray(np.asarray(v, np.float32).reshape(nchunk, 128).T)

    for l in range(DEPTH):
        pb = l * PL
        par[:, pb + P_N1:pb + P_N1 + 8] = fm(inp["ffn1_norm"][l], 8)
        par[:, pb + P_NM:pb + P_NM + 8] = fm(inp["mix_norm"][l], 8)
        par[:, pb + P_N2:pb + P_N2 + 8] = fm(inp["ffn2_norm"][l], 8)
        par[:, pb + P_NP:pb + P_NP + 8] = fm(inp["ple_norm"][l], 8)
        scw = np.asarray(inp["sc_conv_w"][l], np.float32)
        par[:, pb + P_SCW:pb + P_SCW + 24] = scw.reshape(3, 8, 128).transpose(2, 1, 0).reshape(128, 24)
        mcw = np.asarray(inp["m_conv_w"][l], np.float32)
        par[:, pb + P_MCW:pb + P_MCW + 96] = mcw.reshape(4, 24, 128).transpose(2, 1, 0).reshape(128, 96)
        par[:, pb + P_MCB:pb + P_MCB + 24] = fm(inp["m_conv_b"][l], 24)
        par[:, pb + P_MN:pb + P_MN + 16] = fm(inp["m_norm"][l], 16)
        par[:, pb + P_DTB:pb + P_DTB + NH] = np.asarray(inp["m_dt_bias"][l], np.float32)[None, :]
        par[:, pb + P_AL:pb + P_AL + NH] = np.asarray(inp["m_A_log"][l], np.float32)[None, :]
        par[:, pb + P_DD:pb + P_DD + NH] = np.asarray(inp["m_D"][l], np.float32)[None, :]
    par[:, DEPTH * PL:DEPTH * PL + 8] = fm(inp["final_norm"], 8)
    return par


def _consts():
    c = np.zeros((128, 5, 128), np.float32)
    i = np.arange(128)
    c[:, 0, :] = np.eye(128, dtype=np.float32)
    c[:, 1, :] = (i[:, None] <= i[None, :]).astype(np.float32)
    c[:, 2, :] = (i[:, None] > i[None, :]).astype(np.float32)
    c[:, 3, :] = (i[None, :] >= i[:, None]).astype(np.float32)
    c[:, 4, :] = 1.0
    return c


_CACHE = {}


def run_layers(h_in, inp, layers, do_final, S, ncores, trace=False, phases=("ffn1", "mixer", "ffn2", "ple")):
    key = (S, tuple(layers), do_final, tuple(phases))
    if key not in _CACHE:
        _CACHE[key] = build_program(S, list(layers), do_final, phases=phases)
    nc = _CACHE[key]
    par = _pack_params(inp)
    cst = _consts()
    wts = {n: np.ascontiguousarray(np.asarray(inp[n], np.float32)[list(layers)]) for n in WSHAPES}
    p_all = np.asarray(inp["p"], np.float32)
    in_maps = []
    for b in range(ncores):
        m = {"x": np.ascontiguousarray(h_in[b]), "p": np.ascontiguousarray(p_all[:, b, :S, :]), "params": par, "consts": cst}
        m.update(wts)
        in_maps.append(m)
    res = run_bass_kernel_spmd(nc, in_maps, core_ids=list(range(ncores)), **({"trace": True} if trace else {}))
    out = np.stack([np.asarray(r["out"], np.float32) for r in res.results], axis=0)
    return out, res


def kernel(**inputs):
    x = np.asarray(inputs["x"], np.float32)
    out, _ = run_layers(x, inputs, [0, 1, 2, 3], True, 4096, 8)
    return out
```
